# Optimizing a Trainium2 kernel written in Bass

```python
import math
import jax, jax.numpy as jnp
from jax import lax
import numpy as np

D_MODEL = 1024
BATCH = 8
SEQ = 2048
DEPTH = 4

CTX_LEN = 256
GRID_W = 64
SSD_HEADS = 16
SSD_HEAD_DIM = 64
SSD_DIM = SSD_HEADS * SSD_HEAD_DIM
SSD_GROUPS = 2
SSD_STATE = 128
SSD_CONV_W = 5
SSD_CHUNK = 128
XBC_DIM = SSD_DIM + 2 * SSD_GROUPS * SSD_STATE
CM_DIM = D_MODEL
CM_KERNEL = 31
MIX_DIM = SSD_DIM + CM_DIM
IN_DIM = SSD_DIM + XBC_DIM + 2 * SSD_HEADS + 2 * CM_DIM
FFN_HIDDEN = -(-8 * D_MODEL // (3 * 256)) * 256
EPS = 1e-6

kernel_name = "hybrid_ssd_conformer_flow_backbone"


def _rmsnorm(x, g):
    xf = x.astype(jnp.float32)
    y = xf * lax.rsqrt(jnp.mean(xf * xf, axis=-1, keepdims=True) + EPS)
    return y.astype(x.dtype) * g


def _layernorm(x, g, b):
    xf = x.astype(jnp.float32)
    mu = jnp.mean(xf, axis=-1, keepdims=True)
    xc = xf - mu
    var = jnp.mean(xc * xc, axis=-1, keepdims=True)
    return (xc * lax.rsqrt(var + EPS)).astype(x.dtype) * g + b


def _dwconv(x, w, b):
    k = w.shape[0]
    y = lax.conv_general_dilated(
        x, w[:, None, :].astype(x.dtype), window_strides=(1,),
        padding=[(k // 2, k // 2)], dimension_numbers=('NWC', 'WIO', 'NWC'),
        feature_group_count=x.shape[-1])
    return y + b


def _ssd_scan(xs, dt, a, bm, cm, h0):
    f32 = jnp.float32
    b, l, h, p = xs.shape
    g, n = bm.shape[2], bm.shape[3]
    nc = l // SSD_CHUNK
    rep = h // g
    xs, dt, bm, cm = xs.astype(f32), dt.astype(f32), bm.astype(f32), cm.astype(f32)
    bh = jnp.repeat(bm, rep, axis=2).reshape(b, nc, SSD_CHUNK, h, n)
    ch = jnp.repeat(cm, rep, axis=2).reshape(b, nc, SSD_CHUNK, h, n)
    xdt = (xs * dt[..., None]).reshape(b, nc, SSD_CHUNK, h, p)
    a_cs = jnp.cumsum((dt * a).reshape(b, nc, SSD_CHUNK, h), axis=2)
    mask = jnp.tril(jnp.ones((SSD_CHUNK, SSD_CHUNK), dtype=bool))[None, None, :, :, None]
    seg = a_cs[:, :, :, None, :] - a_cs[:, :, None, :, :]
    decay = jnp.exp(jnp.where(mask, seg, -jnp.inf))
    scores = jnp.einsum('bclhn,bcshn->bclsh', ch, bh) * decay
    y_diag = jnp.einsum('bclsh,bcshp->bclhp', scores, xdt)
    decay_to_end = jnp.exp(a_cs[:, :, -1:, :] - a_cs)
    states = jnp.einsum('bclhn,bclh,bclhp->bchpn', bh, decay_to_end, xdt)
    chunk_decay = jnp.exp(a_cs[:, :, -1, :])

    def step(h_prev, inp):
        st, dec = inp
        return h_prev * dec[:, :, None, None] + st, h_prev

    h_final, h_in = lax.scan(step, h0.astype(f32),
                             (jnp.moveaxis(states, 1, 0), jnp.moveaxis(chunk_decay, 1, 0)))
    h_in = jnp.moveaxis(h_in, 0, 1)
    y_off = jnp.einsum('bclhn,bchpn,bclh->bclhp', ch, h_in, jnp.exp(a_cs))
    return (y_diag + y_off).reshape(b, l, h, p), h_final


def _ssd_bidir(xs, dt, a, bm, cm, h0_f, h0_b):
    flip = lambda t: jnp.flip(t, axis=1)
    y_f, h_f = _ssd_scan(xs, dt[:, :, 0, :], a[0], bm, cm, h0_f)
    y_b, h_b = _ssd_scan(flip(xs), flip(dt[:, :, 1, :]), a[1], flip(bm), flip(cm), h0_b)
    return y_f + flip(y_b), h_f, h_b


def _mixer_inputs(h, w_in, conv_w, conv_b, dt_bias):
    b, l = h.shape[0], h.shape[1]
    proj = h @ w_in
    i0 = SSD_DIM
    i1 = i0 + XBC_DIM
    i2 = i1 + 2 * SSD_HEADS
    z, xbc, dt_raw, cm_in = proj[..., :i0], proj[..., i0:i1], proj[..., i1:i2], proj[..., i2:]
    xbc = jax.nn.silu(_dwconv(xbc, conv_w, conv_b))
    gn = SSD_GROUPS * SSD_STATE
    xs = xbc[..., :SSD_DIM].reshape(b, l, SSD_HEADS, SSD_HEAD_DIM)
    bm = xbc[..., SSD_DIM:SSD_DIM + gn].reshape(b, l, SSD_GROUPS, SSD_STATE)
    cm = xbc[..., SSD_DIM + gn:].reshape(b, l, SSD_GROUPS, SSD_STATE)
    dt = jax.nn.softplus(dt_raw.astype(jnp.float32).reshape(b, l, 2, SSD_HEADS)
                         + dt_bias.astype(jnp.float32))
    return z, xs, bm, cm, dt, cm_in


def _ssd_out(y, xs, z, d_skip, norm_g):
    b, l = y.shape[0], y.shape[1]
    y = y + d_skip.astype(jnp.float32)[:, None] * xs.astype(jnp.float32)
    y = y.reshape(b, l, SSD_DIM).astype(z.dtype)
    return _rmsnorm(y * jax.nn.silu(z), norm_g)


def _conv_module(cm_in, dw_w, dw_b, ln_g, ln_b, on_grid):
    a, gate = cm_in[..., :CM_DIM], cm_in[..., CM_DIM:]
    u = a * jax.nn.sigmoid(gate)
    if on_grid:
        b, l, ch = u.shape
        rows = l // GRID_W
        u = _dwconv(u.reshape(b * rows, GRID_W, ch), dw_w, dw_b).reshape(b, l, ch)
    else:
        u = _dwconv(u, dw_w, dw_b)
    return jax.nn.silu(_layernorm(u, ln_g, ln_b))


def _mixer(h_ctx, h_lat, w_in, conv_w, conv_b, dt_bias, a_log, d_skip, ssd_norm_g,
           dw_w, dw_b, ln_g, ln_b, w_out, need_ctx):
    a = -jnp.exp(a_log.astype(jnp.float32))
    zc, xc, bc, cc, dtc, cmc = _mixer_inputs(h_ctx, w_in, conv_w, conv_b, dt_bias)
    zl, xl, bl, cl, dtl, cml = _mixer_inputs(h_lat, w_in, conv_w, conv_b, dt_bias)
    h0 = jnp.zeros((h_ctx.shape[0], SSD_HEADS, SSD_HEAD_DIM, SSD_STATE), jnp.float32)
    yc, hf, hb = _ssd_bidir(xc, dtc, a, bc, cc, h0, h0)
    yl, _, _ = _ssd_bidir(xl, dtl, a, bl, cl, hf, hb)
    lat = jnp.concatenate([_ssd_out(yl, xl, zl, d_skip, ssd_norm_g),
                           _conv_module(cml, dw_w, dw_b, ln_g, ln_b, True)], axis=-1) @ w_out
    if need_ctx:
        ctx_o = jnp.concatenate([_ssd_out(yc, xc, zc, d_skip, ssd_norm_g),
                                 _conv_module(cmc, dw_w, dw_b, ln_g, ln_b, False)], axis=-1) @ w_out
    else:
        ctx_o = None
    return lat, ctx_o


def _swiglu(h, w1, w2):
    hid = h @ w1
    return (jax.nn.silu(hid[..., :FFN_HIDDEN]) * hid[..., FFN_HIDDEN:]) @ w2


def setup_inputs(seed: int = 0) -> dict:
    key = jax.random.key(seed)
    ks = jax.random.split(key, 24)
    nrm = lambda k, shape, s: jax.random.normal(k, shape, jnp.float32) * s
    dt0 = jnp.exp(jax.random.uniform(ks[7], (DEPTH, 2, SSD_HEADS), jnp.float32,
                                     minval=math.log(1e-3), maxval=math.log(1e-1)))
    return {
        'x': nrm(ks[0], (BATCH, SEQ, D_MODEL), 1.0),
        'c': nrm(ks[1], (BATCH, D_MODEL), 1.0),
        'ctx': nrm(ks[2], (BATCH, CTX_LEN, D_MODEL), 1.0),
        'c_ctx': nrm(ks[3], (D_MODEL,), 1.0),
        'w_in': nrm(ks[4], (DEPTH, D_MODEL, IN_DIM), D_MODEL ** -0.5),
        'ssd_conv_w': nrm(ks[5], (DEPTH, SSD_CONV_W, XBC_DIM), SSD_CONV_W ** -0.5),
        'ssd_conv_b': nrm(ks[6], (DEPTH, XBC_DIM), 0.02),
        'dt_bias': dt0 + jnp.log(-jnp.expm1(-dt0)),
        'a_log': jnp.log(jax.random.uniform(ks[8], (DEPTH, 2, SSD_HEADS), jnp.float32,
                                            minval=1.0, maxval=16.0)),
        'd_skip': 1.0 + nrm(ks[9], (DEPTH, SSD_HEADS), 0.1),
        'ssd_norm_g': 1.0 + nrm(ks[10], (DEPTH, SSD_DIM), 0.1),
        'cm_dw_w': nrm(ks[11], (DEPTH, CM_KERNEL, CM_DIM), CM_KERNEL ** -0.5),
        'cm_dw_b': nrm(ks[12], (DEPTH, CM_DIM), 0.02),
        'cm_ln_g': 1.0 + nrm(ks[13], (DEPTH, CM_DIM), 0.1),
        'cm_ln_b': nrm(ks[14], (DEPTH, CM_DIM), 0.02),
        'w_out': nrm(ks[15], (DEPTH, MIX_DIM, D_MODEL), MIX_DIM ** -0.5),
        'w_ffn_in': nrm(ks[16], (DEPTH, D_MODEL, 2 * FFN_HIDDEN), D_MODEL ** -0.5),
        'w_ffn_out': nrm(ks[17], (DEPTH, FFN_HIDDEN, D_MODEL), FFN_HIDDEN ** -0.5),
        'ada_w': nrm(ks[18], (DEPTH, D_MODEL, 6 * D_MODEL), 0.02),
        'ada_b': nrm(ks[19], (DEPTH, 6 * D_MODEL), 0.02),
        'norm1_g': 1.0 + nrm(ks[20], (DEPTH, D_MODEL), 0.1),
        'norm2_g': 1.0 + nrm(ks[21], (DEPTH, D_MODEL), 0.1),
        'final_norm_g': 1.0 + nrm(ks[22], (D_MODEL,), 0.1),
    }


def reference(x, c, ctx, c_ctx, w_in, ssd_conv_w, ssd_conv_b, dt_bias, a_log, d_skip,
              ssd_norm_g, cm_dw_w, cm_dw_b, cm_ln_g, cm_ln_b, w_out, w_ffn_in, w_ffn_out,
              ada_w, ada_b, norm1_g, norm2_g, final_norm_g):
    sc = jax.nn.silu(c)
    sc_ctx = jax.nn.silu(c_ctx)
    for i in range(DEPTH):
        need_ctx = i < DEPTH - 1
        mod_l = jnp.split((sc @ ada_w[i] + ada_b[i])[:, None, :], 6, axis=-1)
        mod_c = jnp.split(sc_ctx @ ada_w[i] + ada_b[i], 6, axis=-1)
        sh1, s1, g1, sh2, s2, g2 = mod_l
        csh1, cs1, cg1, csh2, cs2, cg2 = mod_c
        h_lat = _rmsnorm(x, norm1_g[i]) * (1.0 + s1) + sh1
        h_ctx = _rmsnorm(ctx, norm1_g[i]) * (1.0 + cs1) + csh1
        mix_lat, mix_ctx = _mixer(h_ctx, h_lat, w_in[i], ssd_conv_w[i], ssd_conv_b[i],
                                  dt_bias[i], a_log[i], d_skip[i], ssd_norm_g[i],
                                  cm_dw_w[i], cm_dw_b[i], cm_ln_g[i], cm_ln_b[i], w_out[i],
                                  need_ctx)
        x = x + g1 * mix_lat
        x = x + g2 * _swiglu(_rmsnorm(x, norm2_g[i]) * (1.0 + s2) + sh2, w_ffn_in[i], w_ffn_out[i])
        if need_ctx:
            ctx = ctx + cg1 * mix_ctx
            ctx = ctx + cg2 * _swiglu(_rmsnorm(ctx, norm2_g[i]) * (1.0 + cs2) + csh2,
                                      w_ffn_in[i], w_ffn_out[i])
    return _rmsnorm(x, final_norm_g)
```

```python
import numpy as np
from contextlib import ExitStack
import concourse.bass as bass
import concourse.mybir as mybir
from concourse.bass_utils import run_bass_kernel_spmd

F32 = mybir.dt.float32
BF16 = mybir.dt.bfloat16
AF = mybir.ActivationFunctionType
ALU = mybir.AluOpType
AX = mybir.AxisListType

D = 1024
T = 2304
NT = 18
SEQ = 2048
CTX = 256
IN_DIM = 4640
FFH = 2816
EPS = 1e-6
BLK = [(0, 256), (256, 512), (768, 512), (1280, 512), (1792, 512)]
FWD_ORDER = list(range(18))
BWD_ORDER = [1, 0] + list(range(17, 1, -1))


class Buf:
    __slots__ = ("name", "w", "r")

    def __init__(self, name):
        self.name = name
        self.w = None
        self.r = {}


class Sched:
    def __init__(self, nc, es, n_sp=12, n_pool=4):
        self.nc = nc
        self.eng = {"pe": nc.tensor, "act": nc.scalar, "dve": nc.vector, "pool": nc.gpsimd, "sp": nc.sync}
        self.sem = {}
        self.cnt = {}
        for e in ("pe", "act", "dve", "pool"):
            self.sem[e] = es.enter_context(nc.semaphore("s_" + e))
            self.cnt[e] = 0
        self.dsem = {"sp": [], "pool": []}
        for i in range(n_sp):
            self.dsem["sp"].append([es.enter_context(nc.semaphore("d_sp%d" % i)), 0])
        for i in range(n_pool):
            self.dsem["pool"].append([es.enter_context(nc.semaphore("d_pl%d" % i)), 0])
        self.dnext = {"sp": 0, "pool": 0}
        self.waited = {e: {} for e in self.eng}
        self.semobj = {}
        self.bufs = {}
        self.nwaits = 0
        self.nops = 0

    def B(self, *key):
        b = self.bufs.get(key)
        if b is None:
            b = Buf(key)
            self.bufs[key] = b
        return b

    def _wait(self, e, evs):
        best = {}
        for ev in evs:
            if ev is None:
                continue
            s, v = ev
            k = id(s)
            self.semobj[k] = s
            if best.get(k, 0) < v:
                best[k] = v
        for k, v in best.items():
            if e == "pe" and self.semobj[k] is self.sem["pe"]:
                continue
            if self.waited[e].get(k, 0) < v:
                self.eng[e].wait_ge(self.semobj[k], v)
                self.waited[e][k] = v
                self.nwaits += 1

    def _deps(self, reads, writes):
        evs = []
        for b in reads:
            evs.append(b.w)
        for b in writes:
            evs.append(b.w)
            for k, v in b.r.items():
                evs.append((self.semobj[k], v))
        return evs

    def _mark(self, ev, reads, writes):
        s, v = ev
        k = id(s)
        self.semobj[k] = s
        for b in reads:
            if b.r.get(k, 0) < v:
                b.r[k] = v
        for b in writes:
            b.w = ev
            b.r = {}

    def op(self, e, fn, reads=(), writes=(), sig=True):
        self._wait(e, self._deps(reads, writes))
        ins = fn(self.eng[e])
        self.nops += 1
        if sig:
            self.cnt[e] += 1
            ins.then_inc(self.sem[e], 1)
            ev = (self.sem[e], self.cnt[e])
        else:
            ev = (self.sem[e], self.cnt[e] + 1)
        self._mark(ev, reads, writes)
        return ins

    def dma(self, q, out, in_, reads=(), writes=()):
        lst = self.dsem[q]
        i = self.dnext[q]
        self.dnext[q] = (i + 1) % len(lst)
        slot = lst[i]
        evs = self._deps(reads, writes)
        evs.append((slot[0], slot[1]))
        self._wait(q, evs)
        self.eng[q].dma_start(out=out, in_=in_).then_inc(slot[0], 16)
        self.nops += 1
        slot[1] += 16
        self._mark((slot[0], slot[1]), reads, writes)

    def all_events(self, with_pool=True):
        evs = []
        for x in ("pe", "act", "dve", "pool"):
            evs.append((self.sem[x], self.cnt[x]))
        for q in self.dsem:
            if q == "pool" and not with_pool:
                continue
            for s, v in self.dsem[q]:
                evs.append((s, v))
        return evs

    def barrier(self, pool_waits=False):
        evs = self.all_events(with_pool=False)
        for e in ("pe", "act", "dve", "sp") + (("pool",) if pool_waits else ()):
            self._wait(e, evs)

    def final(self):
        self._wait("sp", self.all_events(with_pool=True))


def build_program(NL=4, dbg=()):
    nc = bass.Bass("TRN2", target_bir_lowering=False)
    es = ExitStack()
    dbg_out = {}
    with es:
        def dram(name, shape, kind="ExternalInput", dt=F32):
            return nc.dram_tensor(name, list(shape), dt, kind=kind).ap()

        x_d = dram("x", [SEQ, D])
        ctx_d = dram("ctx", [CTX, D])
        cc_d = dram("cc", [16, 128])
        w_in_d = dram("w_in", [NL, D, IN_DIM])
        scw_d = dram("ssd_conv_w", [NL, 5, 1536])
        scb_d = dram("ssd_conv_b", [NL, 1536])
        dtb_d = dram("dt_bias", [NL, 32])
        alog_d = dram("a_log", [NL, 32])
        dsk_d = dram("d_skip", [NL, 16])
        sng_d = dram("ssd_norm_g", [NL, D])
        cmw_d = dram("cm_dw_w", [NL, 31, D])
        cmb_d = dram("cm_dw_b", [NL, D])
        lng_d = dram("cm_ln_g", [NL, D])
        lnb_d = dram("cm_ln_b", [NL, D])
        w_out_d = dram("w_out", [NL, 2048, D])
        w1_d = dram("w_ffn_in", [NL, D, 2 * FFH])
        w2_d = dram("w_ffn_out", [NL, FFH, D])
        adaw_d = dram("ada_w", [NL, D, 6 * D])
        adab_d = dram("ada_b", [NL, 6 * D])
        n1g_d = dram("norm1_g", [NL, D])
        n2g_d = dram("norm2_g", [NL, D])
        fng_d = dram("final_norm_g", [8, 128])
        out_d = dram("out", [SEQ, D], kind="ExternalOutput")
        x_scr = dram("x_scr", [128, 8, T], kind="Internal")
        acs_dram = dram("acs_scr", [NT, 2, 16, 128], kind="Internal")

        S = Sched(nc, es)
        B = S.B

        def sb(name, shape, dt):
            return es.enter_context(nc.sbuf_tensor(name, list(shape), dt))

        X = sb("X", [128, 8 * T], F32)
        HB = sb("HB", [128, 8 * T], BF16)
        CB = sb("CB", [128, 8 * T], BF16)
        ARENA = sb("ARENA", [128, 6144], F32)
        WS = [sb("WS%d" % i, [128, 4096], BF16) for i in range(2)]
        dtv = sb("dtv", [128, 576], F32)
        nacol = sb("nacol", [128, 576], F32)
        ea = sb("ea", [128, 576], F32)
        dtdte = sb("dtdte", [128, 576], F32)
        cd = sb("cd", [128, 576], F32)
        ident_bf = sb("ident_bf", [128, 128], BF16)
        identf = sb("identf", [128, 128], F32)
        ones_bf = sb("ones_bf", [128, 128], BF16)
        ones_f = sb("ones_f", [128, 128], F32)
        mask_f = sb("mask_f", [128, 128], F32)
        mask_b = sb("mask_b", [128, 128], F32)
        vec = sb("vec", [128, 408], F32)
        fng = sb("fng", [128, 8], F32)
        stg = [sb("stg%d" % i, [128, 128], F32) for i in range(4)]
        modp = [sb("mod%d" % k, [128, 96], F32) for k in range(2)]
        G1p = [sb("G1_%d" % k, [128, 16], F32) for k in range(2)]
        G2p = [sb("G2_%d" % k, [128, 16], F32) for k in range(2)]
        adaT = [sb("adaT%d" % k, [128, 64], F32) for k in range(2)]
        scv = sb("scv", [128, 16], BF16)
        scf = sb("scf", [128, 16], F32)
        gn_bc = sb("gn_bc", [128, D], F32)
        D_bc = sb("D_bc", [128, 16], F32)
        dtb_bc = sb("dtb_bc", [128, 32], F32)
        a_bc = sb("a_bc", [128, 32], F32)
        epsc = sb("epsc", [128, 1], F32)
        onec = sb("onec", [128, 1], F32)
        ssq = sb("ssq", [128, 18], F32)
        rst = sb("rst", [128, 18], F32)

        banks = [es.enter_context(nc.psum_tensor("bank%d" % i, [128, 512], F32)) for i in range(8)]
        bank_i = [0]

        def bank():
            i = bank_i[0]
            bank_i[0] = (i + 1) % 8
            return banks[i], B("bank", i)

        ws_i = [0]

        def wslot():
            i = ws_i[0]
            ws_i[0] = (i + 1) % 2
            return WS[i], B("ws", i)

        Xv = X[:].rearrange("p (f t) -> p f t", f=8)
        XB = X[:].bitcast(BF16)
        xs_tok = XB[:, 0:9216].rearrange("p (c w) -> p c w", w=512)
        y_st = XB[:, 9216:18432].rearrange("p (c w) -> p c w", w=512)
        g_st = XB[:, 18432:36864].rearrange("p (c w) -> p c w", w=1024)
        HBv = HB[:].rearrange("p (f t) -> p f t", f=8)
        CBv = CB[:].rearrange("p (f t) -> p f t", f=8)
        BT = CB[:, 0:2304]
        CT = CB[:, 2304:4608]
        Btok = CB[:, 4608:6912].rearrange("p (c n) -> p c n", n=128)
        CBf = CB[:].bitcast(F32)
        arow = [CBf[:, 3456:4480].rearrange("p (h l) -> p h l", l=128),
                CBf[:, 4480:5504].rearrange("p (h l) -> p h l", l=128)]
        decay = CBf[:, 5504:6528].rearrange("p (h l) -> p h l", l=128)
        ytmp = CBf[:, 6528:7040]
        ytmp2 = CBf[:, 7040:7552]
        state = CBf[:, 7552:8064]
        szt = CBf[:, 8064:8576]
        xsD = CBf[:, 8576:9088]
        AB = ARENA[:].bitcast(BF16)
        dtv3 = dtv[:].rearrange("p (c h) -> p c h", h=32)
        nacol3 = nacol[:].rearrange("p (c h) -> p c h", h=32)
        ea3 = ea[:].rearrange("p (c h) -> p c h", h=32)
        dtdte3 = dtdte[:].rearrange("p (c h) -> p c h", h=32)
        cd3 = cd[:].rearrange("p (c h) -> p c h", h=32)
        mod3p = [m[:].rearrange("p (c v) -> p c v", v=2) for m in modp]
        G13p = [m[:].rearrange("p (c v) -> p c v", v=2) for m in G1p]
        G23p = [m[:].rearrange("p (c v) -> p c v", v=2) for m in G2p]
        cur = {"par": 0}
        scv3 = scv[:].rearrange("p (v k) -> p k v", v=2)

        def ACT(out, in_, func, reads, writes, **kw):
            S.op("act", lambda e: e.activation(out=out, in_=in_, func=func, **kw), reads, writes)

        def TT(out, in0, in1, op, reads, writes, eng="dve"):
            S.op(eng, lambda e: e.tensor_tensor(out=out, in0=in0, in1=in1, op=op), reads, writes)

        def TS(out, in0, s1, s2, op0, op1, reads, writes, eng="dve"):
            if op1 is None:
                S.op(eng, lambda e: e.tensor_scalar(out=out, in0=in0, scalar1=s1, scalar2=None, op0=op0), reads, writes)
            else:
                S.op(eng, lambda e: e.tensor_scalar(out=out, in0=in0, scalar1=s1, scalar2=s2, op0=op0, op1=op1), reads, writes)

        def STT(out, in0, scalar, in1, op0, op1, reads, writes):
            S.op("dve", lambda e: e.scalar_tensor_tensor(out=out, in0=in0, scalar=scalar, in1=in1, op0=op0, op1=op1), reads, writes)

        def CP(eng, out, in_, reads, writes):
            if eng == "act":
                S.op("act", lambda e: e.copy(out=out, in_=in_), reads, writes)
            else:
                S.op(eng, lambda e: e.tensor_copy(out=out, in_=in_), reads, writes)

        def MM(out, lhsT, rhs, start, stop, reads, writes, sig):
            S.op("pe", lambda e: e.matmul(out=out, lhsT=lhsT, rhs=rhs, start=start, stop=stop), reads, writes, sig=sig)

        def TR(out, in_, ident, reads, writes, sig):
            S.op("pe", lambda e: e.transpose(out=out, in_=in_, identity=ident), reads, writes, sig=sig)

        def dump(name, ap, shape, reads):
            if name in dbg:
                d = dram("dbg_" + name, shape, kind="ExternalOutput", dt=ap.dtype)
                dbg_out[name] = d
                S.dma("sp", d, ap, reads=reads, writes=[B("dbgo", name)])

        def blk_of_tile(c):
            return 0 if c < 2 else 1 + (c - 2) // 4

        bC = B("consts")
        S.op("pool", lambda e: e.memset(ones_f[:], 1.0), writes=[bC])
        S.op("pool", lambda e: e.memset(epsc[:], EPS), writes=[bC])
        S.op("pool", lambda e: e.memset(onec[:], 1.0), writes=[bC])
        S.op("pool", lambda e: e.affine_select(out=mask_f[:], in_=ones_f[:], pattern=[[1, 128]], compare_op=ALU.is_ge,
                                               fill=0.0, base=0, channel_multiplier=-1), reads=[bC], writes=[bC])
        S.op("pool", lambda e: e.affine_select(out=mask_b[:], in_=ones_f[:], pattern=[[-1, 128]], compare_op=ALU.is_ge,
                                               fill=0.0, base=0, channel_multiplier=1), reads=[bC], writes=[bC])
        S.op("pool", lambda e: e.affine_select(out=identf[:], in_=ones_f[:], pattern=[[1, 128]], compare_op=ALU.is_equal,
                                               fill=0.0, base=0, channel_multiplier=-1), reads=[bC], writes=[bC])
        CP("dve", ident_bf[:], identf[:], [bC], [bC])
        CP("dve", ones_bf[:], ones_f[:], [bC], [bC])

        S.dma("sp", stg[0][0:16, :], cc_d, writes=[B("stg", 0)])
        bk, bb = bank()
        TR(bk[:, 0:16], stg[0][0:16, :], identf[0:16, 0:16], [B("stg", 0), bC], [bb], True)
        ACT(scf[:], bk[:, 0:16], AF.Silu, [bb], [B("scv")])
        CP("dve", scv[:], scf[:], [B("scv")], [B("scv")])
        S.dma("sp", stg[1][0:8, :], fng_d, writes=[B("stg", 1)])
        bk, bb = bank()
        TR(bk[:, 0:8], stg[1][0:8, :], identf[0:8, 0:8], [B("stg", 1), bC], [bb], True)
        CP("dve", fng[:], bk[:, 0:8], [bb], [B("fng")])

        inb = [ARENA[:, 0:1024], ARENA[:, 1024:2048]]
        for c in range(NT):
            src = ctx_d[c * 128:(c + 1) * 128, :] if c < 2 else x_d[(c - 2) * 128:(c - 1) * 128, :]
            ib = inb[c % 2]
            S.dma("sp", ib, src, writes=[B("inb", c % 2)])
            for half in range(2):
                bk, bb = bank()
                for j in range(4):
                    f = half * 4 + j
                    TR(bk[:, j * 128:(j + 1) * 128], ib[:, f * 128:(f + 1) * 128], identf[:], [B("inb", c % 2), bC], [bb], j == 3)
                CP("act" if half == 0 else "dve", Xv[:, half * 4:half * 4 + 4, c * 128:(c + 1) * 128],
                   bk[:, 0:512].rearrange("p (a b) -> p a b", b=128), [bb], [B("X", blk_of_tile(c))])
        S.barrier()

        def load_consts(i):
            bst = [B("stg", k) for k in range(4)]
            S.dma("sp", stg[0][0:48, :], adab_d[i].rearrange("(r p) -> r p", p=128), writes=[bst[0]])
            S.dma("sp", stg[0][48:56, :], n1g_d[i].rearrange("(r p) -> r p", p=128), writes=[bst[0]])
            S.dma("sp", stg[0][56:64, :], n2g_d[i].rearrange("(r p) -> r p", p=128), writes=[bst[0]])
            S.dma("sp", stg[0][64:76, :], scb_d[i].rearrange("(r p) -> r p", p=128), writes=[bst[0]])
            S.dma("sp", stg[0][76:84, :], cmb_d[i].rearrange("(r p) -> r p", p=128), writes=[bst[0]])
            S.dma("sp", stg[0][84:92, :], lng_d[i].rearrange("(r p) -> r p", p=128), writes=[bst[0]])
            S.dma("sp", stg[0][92:100, :], lnb_d[i].rearrange("(r p) -> r p", p=128), writes=[bst[0]])
            S.dma("sp", stg[1][0:60, :], scw_d[i].rearrange("k (c p) -> (k c) p", p=128), writes=[bst[1]])
            S.dma("sp", stg[2][0:128, :], cmw_d[i][0:16].rearrange("k (c p) -> (k c) p", p=128), writes=[bst[2]])
            S.dma("sp", stg[3][0:120, :], cmw_d[i][16:31].rearrange("k (c p) -> (k c) p", p=128), writes=[bst[3]])
            bk, bb = bank()
            TR(bk[:, 0:100], stg[0][0:100, :], identf[0:100, 0:100], [bst[0], bC], [bb], False)
            TR(bk[:, 100:160], stg[1][0:60, :], identf[0:60, 0:60], [bst[1], bC], [bb], False)
            TR(bk[:, 160:288], stg[2][0:128, :], identf[:], [bst[2], bC], [bb], False)
            TR(bk[:, 288:408], stg[3][0:120, :], identf[0:120, 0:120], [bst[3], bC], [bb], True)
            CP("dve", vec[:], bk[:, 0:408], [bb], [B("vec")])
            S.dma("sp", gn_bc[:], sng_d[i].partition_broadcast(128), writes=[B("gn_bc")])
            S.dma("sp", D_bc[:], dsk_d[i].partition_broadcast(128), writes=[B("smallbc")])
            S.dma("sp", dtb_bc[:], dtb_d[i].partition_broadcast(128), writes=[B("smallbc")])
            S.dma("sp", a_bc[:], alog_d[i].partition_broadcast(128), writes=[B("smallbc")])
            ACT(a_bc[:], a_bc[:], AF.Exp, [B("smallbc")], [B("smallbc")])
            TS(a_bc[:], a_bc[:], -1.0, None, ALU.mult, None, [B("smallbc")], [B("smallbc")])

        def vcol(j):
            return vec[:, j:j + 1]

        def ada_steps(i, par, arena_slots):
            wv = adaw_d[i].rearrange("(k p) n -> p k n", p=128)
            m3 = mod3p[par]
            aT = adaT[par]
            bM, bG, bT = B("mod", par), B("G", par), B("adaT", par)
            steps = []

            def setup():
                bs = B("stg", 0)
                S.dma("sp", stg[0][0:48, :], adab_d[i].rearrange("(r p) -> r p", p=128), writes=[bs])
                S.dma("sp", stg[0][48:56, :], n1g_d[i].rearrange("(r p) -> r p", p=128), writes=[bs])
                S.dma("sp", stg[0][56:64, :], n2g_d[i].rearrange("(r p) -> r p", p=128), writes=[bs])
                bk, bb = bank()
                TR(bk[:, 0:64], stg[0][0:64, :], identf[0:64, 0:64], [bs, bC], [bb], True)
                CP("dve", aT[:], bk[:, 0:64], [bb], [bT])
            steps.append(setup)

            def mk(s):
                def step():
                    if arena_slots:
                        k_ = s % 2
                        w3 = AB[:, 4096 + k_ * 4096:8192 + k_ * 4096].rearrange("p (k w) -> p k w", w=512)
                        wb = B("adaslot", k_)
                    else:
                        ws, wb = wslot()
                        w3 = ws[:, 0:4096].rearrange("p (k w) -> p k w", w=512)
                    S.dma("pool", w3, wv[:, :, s * 512:(s + 1) * 512], writes=[wb])
                    bk, bb = bank()
                    for j in range(4):
                        for k in range(8):
                            MM(bk[:, j * 2:j * 2 + 2], w3[:, k, j * 128:(j + 1) * 128], scv3[:, k, :],
                               k == 0, k == 7, [wb, B("scv")], [bb], (k == 7 and j == 3))
                    TT(m3[:, s * 4:(s + 1) * 4, :], bk[:, 0:8].rearrange("p (c v) -> p c v", v=2),
                       aT[:, s * 4:(s + 1) * 4].unsqueeze(2).to_broadcast([128, 4, 2]), ALU.add, [bb, bT], [bM])
                return step
            for s in range(12):
                steps.append(mk(s))

            def finish():
                TS(G13p[par], m3[:, 8:16, :], 1.0, None, ALU.add, None, [bM], [bG])
                TT(G13p[par], G13p[par], aT[:, 48:56].unsqueeze(2).to_broadcast([128, 8, 2]), ALU.mult, [bG, bT], [bG])
                TS(G23p[par], m3[:, 32:40, :], 1.0, None, ALU.add, None, [bM], [bG])
                TT(G23p[par], G23p[par], aT[:, 56:64].unsqueeze(2).to_broadcast([128, 8, 2]), ALU.mult, [bG, bT], [bG])
            steps.append(finish)
            return steps

        def rms_block(t0, N, src_buf):
            sq3 = AB[:, 0:4096].rearrange("p (f n) -> p f n", n=512)
            rs = ARENA[:, 2048:2560]
            sd = ARENA[:, 2560:3072]
            ACT(sq3[:, :, 0:N], Xv[:, :, t0:t0 + N], AF.Square, [src_buf], [B("sq")])
            bk, bb = bank()
            for f in range(8):
                MM(bk[:, 0:N], ones_bf[:], sq3[:, f, 0:N], f == 0, f == 7, [B("sq"), bC], [bb], f == 7)
            ACT(sd[:, 0:N], bk[:, 0:N], AF.Sqrt, [bb, bC], [B("sd")], scale=1.0 / D, bias=epsc[:])
            S.op("dve", lambda e: e.reciprocal(out=rs[:, 0:N], in_=sd[:, 0:N]), [B("sd")], [B("rs")])
            return rs

        def norm_stats(b, pb):
            t0, N = BLK[b]
            sq3 = AB[:, pb * 4096:(pb + 1) * 4096].rearrange("p (f n) -> p f n", n=512)
            rs = ARENA[:, 4096 + pb * 512:4608 + pb * 512]
            ACT(sq3[:, :, 0:N], Xv[:, :, t0:t0 + N], AF.Square, [B("X", b)], [B("sq", pb)])
            bk, bb = bank()
            for f in range(8):
                MM(bk[:, 0:N], ones_bf[:], sq3[:, f, 0:N], f == 0, f == 7, [B("sq", pb), bC], [bb], f == 7)
            ACT(rs[:, 0:N], bk[:, 0:N], AF.Sqrt, [bb, bC], [B("rs", pb)], scale=1.0 / D, bias=epsc[:])
            S.op("dve", lambda e: e.reciprocal(out=rs[:, 0:N], in_=rs[:, 0:N]), [B("rs", pb)], [B("rs", pb)])
            return rs

        def norm_apply(which, b, par, rs, pb):
            G3 = G13p[par] if which == 1 else G23p[par]
            mod3 = mod3p[par]
            sh0 = 0 if which == 1 else 24
            tmpf = [ARENA[:, 5120:5632], ARENA[:, 5632:6144]]
            t0, N = BLK[b]
            v = 1 if b == 0 else 0
            for f in range(8):
                tf = tmpf[f % 2]
                TT(tf[:, 0:N], Xv[:, f, t0:t0 + N], rs[:, 0:N], ALU.mult, [B("X", b), B("rs", pb)], [B("tmpf", f % 2)])
                ACT(HBv[:, f, t0:t0 + N], tf[:, 0:N], AF.Identity, [B("tmpf", f % 2), B("G", par), B("mod", par)], [B("HB", b)],
                    scale=G3[:, f, v:v + 1], bias=mod3[:, sh0 + f, v:v + 1])
            if which == 1:
                S.dma("sp", x_scr[:, :, t0:t0 + N], Xv[:, :, t0:t0 + N], reads=[B("X", b)], writes=[B("xscr", b)])

        def norm_block(which, b, par):
            rs = norm_stats(b, b % 2)
            norm_apply(which, b, par, rs, b % 2)

        def norm_phase(which, skip_ctx=False):
            blks = [b for b in range(5) if not (skip_ctx and b == 0)]
            par = cur["par"]
            pend = None
            for n_, b in enumerate(blks):
                rs = norm_stats(b, n_ % 2)
                if pend is not None:
                    norm_apply(which, *pend)
                pend = (pend_b := b, par, rs, n_ % 2)
            norm_apply(which, *pend)

        def dt_phase(i):
            HBall = [B("HB", b) for b in range(5)]
            ws, wb = wslot()
            w3 = ws[:, 0:256].rearrange("p (k w) -> p k w", w=32)
            S.dma("pool", w3, w_in_d[i].rearrange("(k p) n -> p k n", p=128)[:, :, 2560:2592], writes=[wb])
            bkA, bbA = bank()
            bkB, bbB = bank()
            for c in range(NT):
                bk_, bb_ = (bkA, bbA) if c < 16 else (bkB, bbB)
                for k in range(8):
                    MM(bk_[:, (c % 16) * 32:(c % 16) * 32 + 32], HBv[:, k, c * 128:(c + 1) * 128], w3[:, k, :],
                       k == 0, k == 7, [HBall[blk_of_tile(c)], wb], [bb_], k == 7 and (c == 15 or c == 17))
            dA = ARENA[:, 0:576]
            dA3 = dA.rearrange("p (c h) -> p c h", h=32)
            araw = ARENA[:, 576:1152]
            bA = B("dtA")
            TT(araw[:, 0:512].rearrange("p (c h) -> p c h", h=32), bkA[:, 0:512].rearrange("p (c h) -> p c h", h=32),
               dtb_bc[:].unsqueeze(1).to_broadcast([128, 16, 32]), ALU.add, [bbA, B("smallbc")], [bA])
            TT(araw[:, 512:576].rearrange("p (c h) -> p c h", h=32), bkB[:, 0:64].rearrange("p (c h) -> p c h", h=32),
               dtb_bc[:].unsqueeze(1).to_broadcast([128, 2, 32]), ALU.add, [bbB, B("smallbc")], [bA])
            ACT(araw, araw, AF.Exp, [bA], [bA])
            ACT(dtv[:], araw, AF.Ln, [bA, bC], [B("dt")], bias=onec[:], scale=1.0)
            TT(dA3, dtv3, a_bc[:].unsqueeze(1).to_broadcast([128, 18, 32]), ALU.mult, [B("dt"), B("smallbc")], [B("dA")])
            bkT, bbT = bank()
            bkT2, bbT2 = bank()
            MM(bkT[:, 0:512], ones_f[:], dA[:, 0:512], True, True, [B("dA"), bC], [bbT], True)
            MM(bkT2[:, 0:64], ones_f[:], dA[:, 512:576], True, True, [B("dA"), bC], [bbT2], True)
            bkC, bbC = bank()
            bkC2, bbC2 = bank()
            for c in range(NT):
                bk_, bb_ = (bkC, bbC) if c < 16 else (bkC2, bbC2)
                o = (c % 16) * 32
                MM(bk_[:, o:o + 16], mask_f[:], dA3[:, c, 0:16], True, True, [B("dA"), bC], [bb_], False)
                MM(bk_[:, o + 16:o + 32], mask_b[:], dA3[:, c, 16:32], True, True, [B("dA"), bC], [bb_], c == 15 or c == 17)
            TS(nacol[:, 0:512], bkC[:, 0:512], -1.0, None, ALU.mult, None, [bbC], [B("nacol")])
            TS(nacol[:, 512:576], bkC2[:, 0:64], -1.0, None, ALU.mult, None, [bbC2], [B("nacol")])
            ACT(ea[:], nacol[:], AF.Exp, [B("nacol")], [B("ea")], scale=-1.0)
            ACT(cd[:, 0:512], bkT[:, 0:512], AF.Exp, [bbT], [B("cd")])
            ACT(cd[:, 512:576], bkT2[:, 0:64], AF.Exp, [bbT2], [B("cd")])
            TT(dtdte[:, 0:512], bkT[:, 0:512], nacol[:, 0:512], ALU.add, [bbT, B("nacol")], [B("dtdte")])
            TT(dtdte[:, 512:576], bkT2[:, 0:64], nacol[:, 512:576], ALU.add, [bbT2, B("nacol")], [B("dtdte")])
            ACT(dtdte[:], dtdte[:], AF.Exp, [B("dtdte")], [B("dtdte")])
            TT(dtdte[:], dtdte[:], dtv[:], ALU.mult, [B("dtdte"), B("dt")], [B("dtdte")])
            ACT(dtv[:], dtv[:], AF.Ln, [B("dt")], [B("dt")])
            TT(dtv[:], dtv[:], nacol[:], ALU.add, [B("dt"), B("nacol")], [B("dt")])
            sg = [ARENA[0:16, 1152:1408], ARENA[0:16, 1408:1664]]
            for c in range(NT):
                bk_, bb_ = bank()
                MM(bk_[0:16, 0:128], dA3[:, c, 0:16], mask_f[:], True, True, [B("dA"), bC], [bb_], False)
                MM(bk_[0:16, 128:256], dA3[:, c, 16:32], mask_b[:], True, True, [B("dA"), bC], [bb_], True)
                CP("act", sg[c % 2], bk_[0:16, 0:256], [bb_], [B("sg", c % 2)])
                S.dma("sp", acs_dram[c].rearrange("d h l -> h d l"), sg[c % 2].rearrange("h (d l) -> h d l", d=2),
                      reads=[B("sg", c % 2)], writes=[B("acs", c)])

        def ssd_proj(i, g):
            HBall = [B("HB", b) for b in range(5)]
            xpb = [AB[:, 0:2312], AB[:, 2312:4624]]
            dg = [AB[:, 4624:5264].rearrange("p (k m) -> p k m", m=128), AB[:, 5264:5904].rearrange("p (k m) -> p k m", m=128)]
            sob = [AB[:, 5904:8208], AB[:, 8208:10512]]
            for p in range(2):
                S.op("dve", lambda e: e.memset(xpb[p][:, 0:2], 0.0), writes=[B("xp", p)])
                S.op("dve", lambda e: e.memset(xpb[p][:, 258:262], 0.0), writes=[B("xp", p)])
                S.op("dve", lambda e: e.memset(xpb[p][:, 2310:2312], 0.0), writes=[B("xp", p)])
            wv = w_in_d[i].rearrange("(k p) n -> p k n", p=128)
            wsA, wbA = wslot()
            wA3 = wsA[:, 0:4096].rearrange("p (k w) -> p k w", w=512)
            S.dma("pool", wA3, wv[:, :, 1024 + g * 512:1024 + (g + 1) * 512], writes=[wbA])
            wsB, wbB = wslot()
            wB3 = wsB[:, 0:2048].rearrange("p (k w) -> p k w", w=256)
            S.dma("pool", wB3[:, :, 0:128], wv[:, :, 2048 + g * 128:2048 + (g + 1) * 128], writes=[wbB])
            S.dma("pool", wB3[:, :, 128:256], wv[:, :, 2304 + g * 128:2304 + (g + 1) * 128], writes=[wbB])
            chunks = [("xs", q, wA3, wbA, q * 128, g * 4 + q) for q in range(4)]
            chunks.append(("B", 0, wB3, wbB, 0, 8 + g))
            chunks.append(("C", 0, wB3, wbB, 128, 10 + g))
            for ci, (kind, q, w3, wb, col0, qc) in enumerate(chunks):
                p = ci % 2
                xp = xpb[p]
                bxp = B("xp", p)
                for k in range(5):
                    TS(dg[p][:, k, :], ident_bf[:], vcol(100 + k * 12 + qc), None, ALU.mult, None, [bC, B("vec")], [B("dg", p)])
                for b, (t0, N) in enumerate(BLK):
                    bk, bb = bank()
                    for k in range(8):
                        MM(bk[:, 0:N], w3[:, k, col0:col0 + 128], HBv[:, k, t0:t0 + N], k == 0, k == 7, [wb, HBall[b]], [bb], k == 7)
                    off = 2 if b == 0 else 262 + (t0 - 256)
                    CP("dve", xp[:, off:off + N], bk[:, 0:N], [bb], [bxp])
                if kind == "xs":
                    dest, bdest = sob[p], B("so", p)
                elif kind == "B":
                    dest, bdest = BT, B("BT")
                else:
                    dest, bdest = CT, B("CT")
                for b, (t0, N) in enumerate(BLK):
                    bk, bb = bank()
                    base = 0 if b == 0 else 260 + (t0 - 256)
                    for k in range(5):
                        MM(bk[:, 0:N], dg[p][:, k, :], xp[:, base + k:base + k + N], k == 0, k == 4, [B("dg", p), bxp], [bb], k == 4)
                    ACT(dest[:, t0:t0 + N], bk[:, 0:N], AF.Silu, [bb, B("vec")], [bdest], bias=vcol(64 + qc), scale=1.0)
                if kind == "C":
                    continue
                for (c0, c1) in ((0, 8), (8, 16), (16, 18)):
                    bk, bb = bank()
                    bkb = bk[:].bitcast(BF16)
                    for c in range(c0, c1):
                        TR(bkb[:, (c - c0) * 128:(c - c0 + 1) * 128], dest[:, c * 128:(c + 1) * 128], ident_bf[:], [bdest, bC], [bb], c == c1 - 1)
                    n = c1 - c0
                    src3 = bkb[:, 0:n * 128].rearrange("p (a b) -> p a b", b=128)
                    if kind == "xs":
                        CP("dve", xs_tok[:, c0:c1, q * 128:(q + 1) * 128], src3, [bb], [B("xs_tok")])
                    else:
                        CP("dve", Btok[:, c0:c1, :], src3, [bb], [B("Btok")])

        def sweep(i, g, d, order, final, wz3=None, wzb=None):
            HBall = [B("HB", b) for b in range(5)]
            scT = [AB[:, 0:1024].rearrange("p (h l) -> p h l", l=128), AB[:, 1024:2048].rearrange("p (h l) -> p h l", l=128)]
            xdt = [AB[:, 2048:2560], AB[:, 2560:3072]]
            xdte = [AB[:, 3072:3584], AB[:, 3584:4096]]
            st_bf = AB[:, 4096:4608]
            CBTm = [ARENA[:, 2304:2432], ARENA[:, 2432:2560]]
            dec = [ARENA[:, 2560:3584].rearrange("p (h l) -> p h l", l=128), ARENA[:, 3584:4608].rearrange("p (h l) -> p h l", l=128)]
            xsDb = [AB[:, 9216:9728], AB[:, 9728:10240]]
            ytb = AB[:, 10240:10752]
            identD = AB[:, 10752:11776].rearrange("p (h m) -> p h m", m=128)
            if final:
                for j in range(8):
                    TS(identD[:, j, :], ident_bf[:], D_bc[:, g * 8 + j:g * 8 + j + 1], None, ALU.mult, None,
                       [bC, B("smallbc")], [B("identD")])
            mask = mask_f if d == 0 else mask_b
            col0 = d * 16 + g * 8
            n = len(order)
            bk_cb, bb_cb = banks[0], B("bank", 0)
            bk_y = [banks[1], banks[2]]
            bk_st = [banks[3], banks[4]]
            bk_z = [banks[5], banks[6]]
            bk_yo, bb_yo = banks[7], B("bank", 7)

            def load_arow(idx):
                c = order[idx]
                S.dma("sp", arow[idx % 2], acs_dram[c, d, g * 8:(g + 1) * 8, :].partition_broadcast(128),
                      reads=[B("acs", c)], writes=[B("arow", idx % 2)])

            def h3(ap):
                return ap.rearrange("p (h e) -> p h e", e=64)

            def front_a(idx):
                c = order[idx]
                p = idx % 2
                if idx + 1 < n:
                    load_arow(idx + 1)
                ar = arow[p]
                bar = B("arow", p)
                cs = slice(c * 128, (c + 1) * 128)
                MM(bk_cb[:, 0:128], BT[:, cs], CT[:, cs], True, True, [B("BT"), B("CT")], [bb_cb], True)
                TT(CBTm[p], bk_cb[:, 0:128], mask[:], ALU.mult, [bb_cb, bC], [B("CBTm", p)])
                for j in range(8):
                    ACT(dec[p][:, j, :], ar[:, j, :], AF.Exp, [bar, B("dt")], [B("decay", p)],
                        bias=dtv3[:, c, col0 + j:col0 + j + 1], scale=1.0)
                STT(scT[p], dec[p], 1.0e30, CBTm[p].unsqueeze(1).to_broadcast([128, 8, 128]), ALU.min, ALU.mult,
                    [B("decay", p), B("CBTm", p)], [B("scT", p)])
                xs3 = h3(xs_tok[:, c, :])
                TT(h3(xdte[p]), xs3, dtdte3[:, c, col0:col0 + 8].unsqueeze(2).to_broadcast([128, 8, 64]), ALU.mult,
                   [B("xs_tok"), B("dtdte")], [B("xdte", p)])

            def front_b(idx):
                c = order[idx]
                p = idx % 2
                cs = slice(c * 128, (c + 1) * 128)
                bby = B("bank", 1 + p)
                if final:
                    MM(bk_y[p][:, 0:512], ident_bf[:], y_st[:, c, :], True, False, [bC, B("y_st", c)], [bby], False)
                    for j in range(8):
                        MM(bk_y[p][:, j * 64:(j + 1) * 64], identD[:, j, :], xs_tok[:, c, j * 64:(j + 1) * 64], False, False,
                           [B("identD"), B("xs_tok")], [bby], False)
                for j in range(8):
                    MM(bk_y[p][:, j * 64:(j + 1) * 64], scT[p][:, j, :], xs_tok[:, c, j * 64:(j + 1) * 64],
                       (not final), (j == 7 and idx == 0) if final else True,
                       [B("scT", p), B("xs_tok")], [bby], j == 7)
                MM(bk_st[p][:, 0:512], Btok[:, c, :], xdte[p], True, True, [B("Btok"), B("xdte", p)], [B("bank", 3 + p)], True)
                if final:
                    for k in range(8):
                        MM(bk_z[p][:, 0:512], HBv[:, k, cs], wz3[:, k, :], k == 0, k == 7, [HBall[blk_of_tile(c)], wzb], [B("bank", 5 + p)], k == 7)

            def back(idx):
                c = order[idx]
                p = idx % 2
                cs = slice(c * 128, (c + 1) * 128)
                bby = B("bank", 1 + p)
                bbst = B("bank", 3 + p)
                bbz = B("bank", 5 + p)
                bky = bk_y[p]
                ea_b = ea3[:, c, col0:col0 + 8].unsqueeze(2).to_broadcast([128, 8, 64])
                if idx > 0:
                    MM(bk_yo[:, 0:512], CT[:, cs], st_bf, True, True, [B("CT"), B("st_bf")], [bb_yo], True)
                    if final:
                        TT(h3(ytb), h3(bk_yo[:, 0:512]), ea_b, ALU.mult, [bb_yo, B("ea")], [B("ytb")])
                        MM(bky[:, 0:512], ident_bf[:], ytb, False, True, [bC, B("ytb")], [bby], True)
                    else:
                        TT(h3(ytmp), h3(bk_yo[:, 0:512]), ea_b, ALU.mult, [bb_yo, B("ea")], [B("ytmp")])
                if not final:
                    if idx > 0:
                        TT(y_st[:, c, :], ytmp, bky[:, 0:512], ALU.add, [B("ytmp"), bby], [B("y_st", c)])
                    else:
                        CP("dve", y_st[:, c, :], bky[:, 0:512], [bby], [B("y_st", c)])
                else:
                    ACT(szt, bk_z[p][:, 0:512], AF.Tanh, [bbz], [B("szt")], scale=0.5)
                    STT(szt, szt, 1.0, bk_z[p][:, 0:512], ALU.add, ALU.mult, [B("szt"), bbz], [B("szt")])
                    STT(g_st[:, c, g * 512:(g + 1) * 512], szt, 0.5, bky[:, 0:512], ALU.mult, ALU.mult, [bby, B("szt")], [B("g_st", c)])
                if idx + 1 < n:
                    if idx == 0:
                        CP("dve", state, bk_st[p][:, 0:512], [bbst], [B("state")])
                    else:
                        TT(h3(state), h3(state), cd3[:, c, col0:col0 + 8].unsqueeze(2).to_broadcast([128, 8, 64]), ALU.mult,
                           [B("state"), B("cd")], [B("state")])
                        TT(state, state, bk_st[p][:, 0:512], ALU.add, [B("state"), bbst], [B("state")])
                    CP("pool", st_bf, state, [B("state")], [B("st_bf")])

            load_arow(0)
            front_a(0)
            if n > 1:
                front_a(1)
            front_b(0)
            for idx in range(n):
                if idx + 2 < n:
                    front_a(idx + 2)
                if idx + 1 < n:
                    front_b(idx + 1)
                back(idx)

        def cm_phase(i):
            NPE = 24
            HBall = [B("HB", b) for b in range(5)]
            upad = [AB[:, 0:3296], AB[:, 3296:6592]]
            diag = AB[:, 6592:6592 + NPE * 128].rearrange("p (k m) -> p k m", m=128)
            accd = ARENA[:, 4832:5344]
            sgm = ARENA[:, 5344:5856]
            S.op("dve", lambda e: e.memset(AB[:, 0:6592], 0.0), writes=[B("upad", 0), B("upad", 1)])
            wv = w_in_d[i].rearrange("(k p) n -> p k n", p=128)
            for q in range(8):
                ws, wb = wslot()
                w3 = ws[:, 0:2048].rearrange("p (k w) -> p k w", w=256)
                S.dma("pool", w3[:, :, 0:128], wv[:, :, 2592 + q * 128:2592 + (q + 1) * 128], writes=[wb])
                S.dma("pool", w3[:, :, 128:256], wv[:, :, 3616 + q * 128:3616 + (q + 1) * 128], writes=[wb])
                up = upad[q % 2]
                bup = B("upad", q % 2)
                up_ctx = up[:, 0:286]
                up_lat = up[:, 286:3294].rearrange("p (r w) -> p r w", w=94)
                for k in range(NPE):
                    TS(diag[:, k, :], ident_bf[:], vcol(160 + k * 8 + q), None, ALU.mult, None, [bC, B("vec")], [B("diag")])
                for b, (t0, N) in enumerate(BLK):
                    bka, bba = bank()
                    bkg, bbg = bank()
                    for k in range(8):
                        MM(bka[:, 0:N], w3[:, k, 0:128], HBv[:, k, t0:t0 + N], k == 0, k == 7, [wb, HBall[b]], [bba], k == 7)
                    for k in range(8):
                        MM(bkg[:, 0:N], w3[:, k, 128:256], HBv[:, k, t0:t0 + N], k == 0, k == 7, [wb, HBall[b]], [bbg], k == 7)
                    ACT(sgm[:, 0:N], bkg[:, 0:N], AF.Sigmoid, [bbg], [B("sgm")])
                    if b == 0:
                        TT(up_ctx[:, 15:271], bka[:, 0:256], sgm[:, 0:256], ALU.mult, [bba, B("sgm")], [bup])
                    else:
                        r0 = (t0 - 256) // 64
                        TT(up_lat[:, r0:r0 + 8, 15:79], bka[:, 0:512].rearrange("p (r w) -> p r w", w=64),
                           sgm[:, 0:512].rearrange("p (r w) -> p r w", w=64), ALU.mult, [bba, B("sgm")], [bup])
                for b, (t0, N) in enumerate(BLK):
                    bkv, bbv = bank()

                    def win(k):
                        if b == 0:
                            return up_ctx[:, k:k + 256]
                        r0 = (t0 - 256) // 64
                        return up_lat[:, r0:r0 + 8, k:k + 64]

                    def shp(ap):
                        return ap[:, 0:256] if b == 0 else ap[:, 0:512].rearrange("p (r w) -> p r w", w=64)
                    for k in range(NPE):
                        MM(shp(bkv), diag[:, k, :], win(k), k == 0, k == NPE - 1, [B("diag"), bup], [bbv], k == NPE - 1)
                    for k in range(NPE, 31):
                        wk = vcol(160 + k * 8 + q)
                        if k == NPE:
                            TS(shp(accd), win(k), wk, None, ALU.mult, None, [bup, B("vec")], [B("accd")])
                        else:
                            STT(shp(accd), win(k), wk, shp(accd), ALU.mult, ALU.add, [bup, B("accd"), B("vec")], [B("accd")])
                    STT(CBv[:, q, t0:t0 + N], bkv[:, 0:N], vcol(76 + q), accd[:, 0:N], ALU.add, ALU.add,
                        [bbv, B("vec"), B("accd")], [B("cm", q, b)])

        def cm_ln():
            vsq = AB[:, 0:4096].rearrange("p (f n) -> p f n", n=512)
            mean = ARENA[:, 2048:2560]
            rstd = ARENA[:, 2560:3072]
            var = ARENA[:, 3072:3584]
            tt = [ARENA[:, 3584:4096], ARENA[:, 4096:4608]]

            def ln_block(b):
                t0, N = BLK[b]
                cmall = [B("cm", q, b) for q in range(8)]
                ACT(vsq[:, :, 0:N], CBv[:, :, t0:t0 + N], AF.Square, cmall, [B("vsq")])
                bk1, bb1 = bank()
                bk2, bb2 = bank()
                for f in range(8):
                    MM(bk1[:, 0:N], ones_bf[:], CBv[:, f, t0:t0 + N], f == 0, f == 7, [cmall[f], bC], [bb1], f == 7)
                for f in range(8):
                    MM(bk2[:, 0:N], ones_bf[:], vsq[:, f, 0:N], f == 0, f == 7, [B("vsq"), bC], [bb2], f == 7)
                TS(mean[:, 0:N], bk1[:, 0:N], 1.0 / D, None, ALU.mult, None, [bb1], [B("mean")])
                TT(var[:, 0:N], mean[:, 0:N], mean[:, 0:N], ALU.mult, [B("mean")], [B("var")])
                STT(var[:, 0:N], bk2[:, 0:N], 1.0 / D, var[:, 0:N], ALU.mult, ALU.subtract, [bb2, B("var")], [B("var")])
                ACT(var[:, 0:N], var[:, 0:N], AF.Sqrt, [B("var"), bC], [B("var")], bias=epsc[:], scale=1.0)
                S.op("dve", lambda e: e.reciprocal(out=rstd[:, 0:N], in_=var[:, 0:N]), [B("var")], [B("rstd")])
                for f in range(8):
                    t_ = tt[f % 2]
                    TT(t_[:, 0:N], CBv[:, f, t0:t0 + N], mean[:, 0:N], ALU.subtract, [cmall[f], B("mean")], [B("tt", f % 2)])
                    TT(t_[:, 0:N], t_[:, 0:N], rstd[:, 0:N], ALU.mult, [B("tt", f % 2), B("rstd")], [B("tt", f % 2)])
                    ACT(CBv[:, f, t0:t0 + N], t_[:, 0:N], AF.Silu, [B("tt", f % 2), B("vec")], [cmall[f]],
                        scale=vcol(84 + f), bias=vcol(92 + f))
            return ln_block

        def ssd_final():
            sqts = [ARENA[:, 2048:3072], ARENA[:, 3072:4096]]
            for c in range(NT):
                sqt = sqts[c % 2]
                ACT(sqt, g_st[:, c, :], AF.Square, [B("g_st", c)], [B("sqt", c % 2)])
                S.op("dve", lambda e: e.reduce_sum(out=ssq[:, c:c + 1], in_=sqt, axis=AX.X), [B("sqt", c % 2)], [B("ssq", c)])
            ACT(rst[:], ssq[:], AF.Sqrt, [B("ssq", c_) for c_ in range(NT)] + [bC], [B("rst")], scale=1.0 / D, bias=epsc[:])
            S.op("dve", lambda e: e.reciprocal(out=rst[:], in_=rst[:]), [B("rst")], [B("rst")])
            ob = [AB[:, 0:1024], AB[:, 1024:2048]]
            for c in range(NT):
                o = ob[c % 2]
                STT(o, g_st[:, c, :], rst[:, c:c + 1], gn_bc[:], ALU.mult, ALU.mult, [B("g_st", c), B("rst"), B("gn_bc")], [B("ob", c % 2)])
                bk, bb = bank()
                bkb = bk[:].bitcast(BF16)
                for f in range(8):
                    TR(bkb[:, f * 128:(f + 1) * 128], o[:, f * 128:(f + 1) * 128], ident_bf[:], [B("ob", c % 2), bC], [bb], f == 7)
                CP("act", HBv[:, :, c * 128:(c + 1) * 128], bkb[:, 0:1024].rearrange("p (a b) -> p a b", b=128), [bb], [B("HB", blk_of_tile(c))])

        def x_reload():
            for b, (t0, N) in enumerate(BLK):
                S.dma("sp", Xv[:, :, t0:t0 + N], x_scr[:, :, t0:t0 + N], reads=[B("xscr", b)], writes=[B("X", b)])

        def wout_phase(i, last, ln_block):
            par = cur["par"]
            mod3 = mod3p[par]
            wv = w_out_d[i].rearrange("(k p) n -> p k n", p=128)

            def units(half):
                for fq in range(2):
                    ws, wb = wslot()
                    w3 = ws[:, 0:4096].rearrange("p (k w) -> p k w", w=512)
                    S.dma("pool", w3, wv[:, half * 8:half * 8 + 8, fq * 512:(fq + 1) * 512], writes=[wb])
                    for f4 in range(4):
                        f = fq * 4 + f4
                        for b_, (t0, N) in enumerate(BLK):
                            if last and b_ == 0:
                                continue
                            yield (w3, wb, f, f4, b_, t0, N)

            def do_unit(half, u):
                w3, wb, f, f4, b_, t0, N = u
                v = 1 if b_ == 0 else 0
                bk, bb = bank()
                for k in range(8):
                    if half == 0:
                        rhs, rb = HBv[:, k, t0:t0 + N], B("HB", b_)
                    else:
                        rhs, rb = CBv[:, k, t0:t0 + N], B("cm", k, b_)
                    MM(bk[:, 0:N], w3[:, k, f4 * 128:(f4 + 1) * 128], rhs, k == 0, k == 7, [wb, rb], [bb], k == 7)
                STT(Xv[:, f, t0:t0 + N], bk[:, 0:N], mod3[:, 16 + f, v:v + 1], Xv[:, f, t0:t0 + N], ALU.mult, ALU.add,
                    [bb, B("mod", par), B("X", b_)], [B("X", b_)])

            ua = list(units(0))
            per = (len(ua) + 4) // 5
            for b_ in range(5):
                ln_block(b_)
                for u in ua[b_ * per:(b_ + 1) * per]:
                    do_unit(0, u)
            for u in units(1):
                do_unit(1, u)

        def ffn_phase(i, last, hooks=(), next_norm=None):
            par = cur["par"]
            mod3 = mod3p[par]
            hooks = list(hooks)

            def hook():
                if hooks:
                    hooks.pop(0)()
            sa = [ARENA[:, 0:512], ARENA[:, 512:1024]]
            first = [True]
            w1v = w1_d[i].rearrange("(k p) n -> p k n", p=128)
            w2v = w2_d[i].rearrange("(j p) n -> p j n", p=128)
            cnt = 0
            for (j0, j1) in ((0, 8), (8, 15), (15, 22)):
                nj = j1 - j0
                for jj in range(j0, j1, 2):
                    w = min(2, j1 - jj)
                    ws, wb = wslot()
                    w3 = ws[:, 0:8 * 2 * w * 128].rearrange("p (k w) -> p k w", w=2 * w * 128)
                    S.dma("pool", w3[:, :, 0:w * 128], w1v[:, :, jj * 128:(jj + w) * 128], writes=[wb])
                    S.dma("pool", w3[:, :, w * 128:2 * w * 128], w1v[:, :, FFH + jj * 128:FFH + (jj + w) * 128], writes=[wb])
                    if first[0]:
                        first[0] = False
                        if hooks:
                            S._wait("pool", S.all_events(with_pool=False))
                    for jw in range(w):
                        j = jj + jw
                        for b, (t0, N) in enumerate(BLK):
                            if last and b == 0:
                                continue
                            bka, bba = bank()
                            bkb_, bbb = bank()
                            for k in range(8):
                                MM(bka[:, 0:N], w3[:, k, jw * 128:(jw + 1) * 128], HBv[:, k, t0:t0 + N], k == 0, k == 7, [wb, B("HB", b)], [bba], k == 7)
                            for k in range(8):
                                MM(bkb_[:, 0:N], w3[:, k, (w + jw) * 128:(w + jw + 1) * 128], HBv[:, k, t0:t0 + N], k == 0, k == 7, [wb, B("HB", b)], [bbb], k == 7)
                            s_ = sa[cnt % 2]
                            ACT(s_[:, 0:N], bka[:, 0:N], AF.Silu, [bba], [B("sa", cnt % 2)])
                            TT(CBv[:, j - j0, t0:t0 + N], s_[:, 0:N], bkb_[:, 0:N], ALU.mult, [B("sa", cnt % 2), bbb], [B("hid", j - j0, b)])
                            cnt += 1
                    hook()
                if j1 == 22:
                    S.barrier()
                    wq = []
                    for half in range(2):
                        ws, wb = wslot()
                        w4 = ws[:, 0:2 * nj * 256].rearrange("p (s j w) -> p s j w", s=2, w=256)
                        for s_ in range(2):
                            fp = half * 2 + s_
                            S.dma("pool", w4[:, s_, :, :], w2v[:, j0:j1, fp * 256:(fp + 1) * 256], writes=[wb])
                            wq.append((w4[:, s_, :, :], wb))
                    for b, (t0, N) in enumerate(BLK):
                        if last and b == 0:
                            continue
                        v = 1 if b == 0 else 0
                        for f in range(8):
                            w3, wb = wq[f // 2]
                            f2 = f % 2
                            bk, bb = bank()
                            for jl in range(nj):
                                MM(bk[:, 0:N], w3[:, jl, f2 * 128:(f2 + 1) * 128], CBv[:, jl, t0:t0 + N], jl == 0, jl == nj - 1,
                                   [wb, B("hid", jl, b)], [bb], jl == nj - 1)
                            STT(Xv[:, f, t0:t0 + N], bk[:, 0:N], mod3[:, 40 + f, v:v + 1], Xv[:, f, t0:t0 + N], ALU.mult, ALU.add,
                                [bb, B("mod", par), B("X", b)], [B("X", b)])
                        if next_norm is not None:
                            next_norm(b)
                    S.barrier()
                    continue
                for fp in range(4):
                    ws, wb = wslot()
                    w3 = ws[:, 0:nj * 256].rearrange("p (j w) -> p j w", w=256)
                    S.dma("pool", w3, w2v[:, j0:j1, fp * 256:(fp + 1) * 256], writes=[wb])
                    for f2 in range(2):
                        f = fp * 2 + f2
                        for b, (t0, N) in enumerate(BLK):
                            if last and b == 0:
                                continue
                            v = 1 if b == 0 else 0
                            bk, bb = bank()
                            for jl in range(nj):
                                MM(bk[:, 0:N], w3[:, jl, f2 * 128:(f2 + 1) * 128], CBv[:, jl, t0:t0 + N], jl == 0, jl == nj - 1,
                                   [wb, B("hid", jl, b)], [bb], jl == nj - 1)
                            STT(Xv[:, f, t0:t0 + N], bk[:, 0:N], mod3[:, 40 + f, v:v + 1], Xv[:, f, t0:t0 + N], ALU.mult, ALU.add,
                                [bb, B("mod", par), B("X", b)], [B("X", b)])
                    hook()
                if j1 == 15:
                    while hooks:
                        hook()
                S.barrier()

        def final_phase():
            tbs = [CBf[:, 0:4096].rearrange("p (f n) -> p f n", n=512), CBf[:, 4096:8192].rearrange("p (f n) -> p f n", n=512)]
            ot = [ARENA[:, 4096:5120], ARENA[:, 5120:6144]]
            for b, (t0, N) in enumerate(BLK):
                if b == 0:
                    continue
                rs = rms_block(t0, N, B("X", b))
                tb = tbs[b % 2]
                for f in range(8):
                    STT(tb[:, f, :], Xv[:, f, t0:t0 + N], fng[:, f:f + 1], rs[:, 0:N], ALU.mult, ALU.mult,
                        [B("X", b), B("rs"), B("fng")], [B("tb", b % 2)])
                for cl in range(4):
                    c = (t0 // 128) + cl
                    o = ot[c % 2]
                    for half in range(2):
                        bk, bb = bank()
                        for j in range(4):
                            TR(bk[:, j * 128:(j + 1) * 128], tb[:, half * 4 + j, cl * 128:(cl + 1) * 128], identf[:],
                               [B("tb", b % 2), bC], [bb], j == 3)
                        CP("act" if half == 0 else "dve", o[:, half * 512:(half + 1) * 512], bk[:, 0:512], [bb], [B("ot", c % 2)])
                    S.dma("sp", out_d[(c - 2) * 128:(c - 1) * 128, :], o, reads=[B("ot", c % 2)], writes=[B("outd", c)])

        for st_ in ada_steps(0, 0, False):
            st_()
        for i in range(NL):
            last = (i == NL - 1)
            cur["par"] = i % 2
            load_consts(i)
            norm_phase(1)
            S.barrier()
            dump("hT%d" % i, HB[:], [128, 8 * T], [B("HB", b) for b in range(5)])
            dt_phase(i)
            dump("dtv%d" % i, dtv[:], [128, 576], [B("dt")])
            dump("nacol%d" % i, nacol[:], [128, 576], [B("nacol")])
            S.barrier()
            for g in range(2):
                ssd_proj(i, g)
                if g == 0:
                    dump("xs_tok%d" % i, XB[:, 0:9216], [128, 9216], [B("xs_tok")])
                    dump("BT%d" % i, BT, [128, 2304], [B("BT")])
                    dump("CT%d" % i, CT, [128, 2304], [B("CT")])
                S.barrier(pool_waits=True)
                sweep(i, g, 1, BWD_ORDER, False)
                wz, wzb = wslot()
                wz3 = wz[:, 0:4096].rearrange("p (k w) -> p k w", w=512)
                S.dma("pool", wz3, w_in_d[i].rearrange("(k p) n -> p k n", p=128)[:, :, g * 512:(g + 1) * 512], writes=[wzb])
                sweep(i, g, 0, FWD_ORDER, True, wz3, wzb)
                S.barrier()
            dump("g_st%d" % i, XB[:, 18432:36864], [128, 18432], [B("g_st", c) for c in range(NT)])
            cm_phase(i)
            S.barrier()
            ssd_final()
            dump("ssdT%d" % i, HB[:], [128, 8 * T], [B("HB", b) for b in range(5)])
            S.barrier()
            x_reload()
            wout_phase(i, last, cm_ln())
            dump("cm%d" % i, CB[:], [128, 8 * T], [B("cm", q, b) for q in range(8) for b in range(5)])
            S.barrier()
            dump("x1_%d" % i, X[:], [128, 8 * T], [B("X", b) for b in range(5)])
            norm_phase(2, skip_ctx=last)
            S.barrier()
            ffn_phase(i, last, ada_steps(i + 1, (i + 1) % 2, True) if not last else (),
                      None)
            dump("x2_%d" % i, X[:], [128, 8 * T], [B("X", b) for b in range(5)])
        S.barrier()
        final_phase()
        S.final()
        print("[kernel] ops=%d waits=%d" % (S.nops, S.nwaits))
    return nc, dbg_out


_W_NAMES = ["w_in", "ssd_conv_w", "ssd_conv_b", "dt_bias", "a_log", "d_skip", "ssd_norm_g", "cm_dw_w", "cm_dw_b",
            "cm_ln_g", "cm_ln_b", "w_out", "w_ffn_in", "w_ffn_out", "ada_w", "ada_b", "norm1_g", "norm2_g"]


def make_in_maps(inputs, NL=4):
    f = lambda a: np.ascontiguousarray(np.asarray(a, dtype=np.float32))
    shared = {}
    for n in _W_NAMES:
        a = f(inputs[n])[:NL]
        if n in ("dt_bias", "a_log"):
            a = a.reshape(NL, 32)
        shared[n] = np.ascontiguousarray(a)
    shared["final_norm_g"] = f(inputs["final_norm_g"]).reshape(8, 128)
    x = f(inputs["x"])
    ctx = f(inputs["ctx"])
    c = f(inputs["c"])
    c_ctx = f(inputs["c_ctx"])
    maps = []
    for b in range(8):
        m = dict(shared)
        m["x"] = x[b]
        m["ctx"] = ctx[b]
        m["cc"] = np.ascontiguousarray(np.concatenate([c[b].reshape(8, 128), c_ctx.reshape(8, 128)], axis=0))
        maps.append(m)
    return maps


_CACHE = {}


def kernel(**inputs):
    if "nc" not in _CACHE:
        _CACHE["nc"] = build_program(4)[0]
    nc = _CACHE["nc"]
    maps = make_in_maps(inputs, 4)
    res = run_bass_kernel_spmd(nc, maps, core_ids=list(range(8)))
    out = np.stack([np.asarray(r["out"], dtype=np.float32) for r in res.results], axis=0)
    return out
```

```python
import numpy as np
from contextlib import ExitStack
import concourse.bass as bass
import concourse.mybir as mybir
from concourse.bass_utils import run_bass_kernel_spmd

F32 = mybir.dt.float32
BF16 = mybir.dt.bfloat16
AF = mybir.ActivationFunctionType
ALU = mybir.AluOpType
AX = mybir.AxisListType

D = 1024
T = 2304
NT = 18
SEQ = 2048
CTX = 256
IN_DIM = 4640
FFH = 2816
EPS = 1e-6
BLK = [(0, 256), (256, 512), (768, 512), (1280, 512), (1792, 512)]
FWD_ORDER = list(range(18))
BWD_ORDER = [1, 0] + list(range(17, 1, -1))


class Buf:
    __slots__ = ("name", "w", "r")

    def __init__(self, name):
        self.name = name
        self.w = None
        self.r = {}


class Sched:
    def __init__(self, nc, es, n_sp=12, n_pool=4):
        self.nc = nc
        self.eng = {"pe": nc.tensor, "act": nc.scalar, "dve": nc.vector, "pool": nc.gpsimd, "sp": nc.sync}
        self.sem = {}
        self.cnt = {}
        for e in ("pe", "act", "dve", "pool"):
            self.sem[e] = es.enter_context(nc.semaphore("s_" + e))
            self.cnt[e] = 0
        self.dsem = {"sp": [], "pool": []}
        for i in range(n_sp):
            self.dsem["sp"].append([es.enter_context(nc.semaphore("d_sp%d" % i)), 0])
        for i in range(n_pool):
            self.dsem["pool"].append([es.enter_context(nc.semaphore("d_pl%d" % i)), 0])
        self.dnext = {"sp": 0, "pool": 0}
        self.waited = {e: {} for e in self.eng}
        self.semobj = {}
        self.bufs = {}
        self.nwaits = 0
        self.nops = 0

    def B(self, *key):
        b = self.bufs.get(key)
        if b is None:
            b = Buf(key)
            self.bufs[key] = b
        return b

    def _wait(self, e, evs):
        best = {}
        for ev in evs:
            if ev is None:
                continue
            s, v = ev
            k = id(s)
            self.semobj[k] = s
            if best.get(k, 0) < v:
                best[k] = v
        for k, v in best.items():
            if e == "pe" and self.semobj[k] is self.sem["pe"]:
                continue
            if self.waited[e].get(k, 0) < v:
                self.eng[e].wait_ge(self.semobj[k], v)
                self.waited[e][k] = v
                self.nwaits += 1

    def _deps(self, reads, writes):
        evs = []
        for b in reads:
            evs.append(b.w)
        for b in writes:
            evs.append(b.w)
            for k, v in b.r.items():
                evs.append((self.semobj[k], v))
        return evs

    def _mark(self, ev, reads, writes):
        s, v = ev
        k = id(s)
        self.semobj[k] = s
        for b in reads:
            if b.r.get(k, 0) < v:
                b.r[k] = v
        for b in writes:
            b.w = ev
            b.r = {}

    def op(self, e, fn, reads=(), writes=(), sig=True):
        self._wait(e, self._deps(reads, writes))
        ins = fn(self.eng[e])
        self.nops += 1
        if sig:
            self.cnt[e] += 1
            ins.then_inc(self.sem[e], 1)
            ev = (self.sem[e], self.cnt[e])
        else:
            ev = (self.sem[e], self.cnt[e] + 1)
        self._mark(ev, reads, writes)
        return ins

    def dma(self, q, out, in_, reads=(), writes=()):
        lst = self.dsem[q]
        i = self.dnext[q]
        self.dnext[q] = (i + 1) % len(lst)
        slot = lst[i]
        evs = self._deps(reads, writes)
        evs.append((slot[0], slot[1]))
        self._wait(q, evs)
        self.eng[q].dma_start(out=out, in_=in_).then_inc(slot[0], 16)
        self.nops += 1
        slot[1] += 16
        self._mark((slot[0], slot[1]), reads, writes)

    def all_events(self, with_pool=True):
        evs = []
        for x in ("pe", "act", "dve", "pool"):
            evs.append((self.sem[x], self.cnt[x]))
        for q in self.dsem:
            if q == "pool" and not with_pool:
                continue
            for s, v in self.dsem[q]:
                evs.append((s, v))
        return evs

    def barrier(self, pool_waits=False):
        evs = self.all_events(with_pool=False)
        for e in ("pe", "act", "dve", "sp") + (("pool",) if pool_waits else ()):
            self._wait(e, evs)

    def final(self):
        self._wait("sp", self.all_events(with_pool=True))


def build_program(NL=4, dbg=()):
    nc = bass.Bass("TRN2", target_bir_lowering=False)
    es = ExitStack()
    dbg_out = {}
    with es:
        def dram(name, shape, kind="ExternalInput", dt=F32):
            return nc.dram_tensor(name, list(shape), dt, kind=kind).ap()

        x_d = dram("x", [SEQ, D])
        ctx_d = dram("ctx", [CTX, D])
        cc_d = dram("cc", [16, 128])
        w_in_d = dram("w_in", [NL, D, IN_DIM])
        scw_d = dram("ssd_conv_w", [NL, 5, 1536])
        scb_d = dram("ssd_conv_b", [NL, 1536])
        dtb_d = dram("dt_bias", [NL, 32])
        alog_d = dram("a_log", [NL, 32])
        dsk_d = dram("d_skip", [NL, 16])
        sng_d = dram("ssd_norm_g", [NL, D])
        cmw_d = dram("cm_dw_w", [NL, 31, D])
        cmb_d = dram("cm_dw_b", [NL, D])
        lng_d = dram("cm_ln_g", [NL, D])
        lnb_d = dram("cm_ln_b", [NL, D])
        w_out_d = dram("w_out", [NL, 2048, D])
        w1_d = dram("w_ffn_in", [NL, D, 2 * FFH])
        w2_d = dram("w_ffn_out", [NL, FFH, D])
        adaw_d = dram("ada_w", [NL, D, 6 * D])
        adab_d = dram("ada_b", [NL, 6 * D])
        n1g_d = dram("norm1_g", [NL, D])
        n2g_d = dram("norm2_g", [NL, D])
        fng_d = dram("final_norm_g", [8, 128])
        out_d = dram("out", [SEQ, D], kind="ExternalOutput")
        x_scr = dram("x_scr", [128, 8, T], kind="Internal")
        acs_dram = dram("acs_scr", [NT, 2, 16, 128], kind="Internal")

        S = Sched(nc, es)
        B = S.B

        def sb(name, shape, dt):
            return es.enter_context(nc.sbuf_tensor(name, list(shape), dt))

        X = sb("X", [128, 8 * T], F32)
        HB = sb("HB", [128, 8 * T], BF16)
        CB = sb("CB", [128, 8 * T], BF16)
        ARENA = sb("ARENA", [128, 6144], F32)
        WS = [sb("WS%d" % i, [128, 4096], BF16) for i in range(2)]
        dtv = sb("dtv", [128, 576], F32)
        nacol = sb("nacol", [128, 576], F32)
        ea = sb("ea", [128, 576], F32)
        dtdte = sb("dtdte", [128, 576], F32)
        cd = sb("cd", [128, 576], F32)
        ident_bf = sb("ident_bf", [128, 128], BF16)
        identf = sb("identf", [128, 128], F32)
        ones_bf = sb("ones_bf", [128, 128], BF16)
        ones_f = sb("ones_f", [128, 128], F32)
        mask_f = sb("mask_f", [128, 128], F32)
        mask_b = sb("mask_b", [128, 128], F32)
        vec = sb("vec", [128, 408], F32)
        fng = sb("fng", [128, 8], F32)
        stg = [sb("stg%d" % i, [128, 128], F32) for i in range(4)]
        modp = [sb("mod%d" % k, [128, 96], F32) for k in range(2)]
        G1p = [sb("G1_%d" % k, [128, 16], F32) for k in range(2)]
        G2p = [sb("G2_%d" % k, [128, 16], F32) for k in range(2)]
        adaT = [sb("adaT%d" % k, [128, 64], F32) for k in range(2)]
        scv = sb("scv", [128, 16], BF16)
        scf = sb("scf", [128, 16], F32)
        gn_bc = sb("gn_bc", [128, D], F32)
        D_bc = sb("D_bc", [128, 16], F32)
        dtb_bc = sb("dtb_bc", [128, 32], F32)
        a_bc = sb("a_bc", [128, 32], F32)
        epsc = sb("epsc", [128, 1], F32)
        onec = sb("onec", [128, 1], F32)
        ssq = sb("ssq", [128, 18], F32)
        rst = sb("rst", [128, 18], F32)

        banks = [es.enter_context(nc.psum_tensor("bank%d" % i, [128, 512], F32)) for i in range(8)]
        bank_i = [0]

        def bank():
            i = bank_i[0]
            bank_i[0] = (i + 1) % 8
            return banks[i], B("bank", i)

        ws_i = [0]

        def wslot():
            i = ws_i[0]
            ws_i[0] = (i + 1) % 2
            return WS[i], B("ws", i)

        Xv = X[:].rearrange("p (f t) -> p f t", f=8)
        XB = X[:].bitcast(BF16)
        xs_tok = XB[:, 0:9216].rearrange("p (c w) -> p c w", w=512)
        y_st = XB[:, 9216:18432].rearrange("p (c w) -> p c w", w=512)
        g_st = XB[:, 18432:36864].rearrange("p (c w) -> p c w", w=1024)
        HBv = HB[:].rearrange("p (f t) -> p f t", f=8)
        CBv = CB[:].rearrange("p (f t) -> p f t", f=8)
        BT = CB[:, 0:2304]
        CT = CB[:, 2304:4608]
        Btok = CB[:, 4608:6912].rearrange("p (c n) -> p c n", n=128)
        CBf = CB[:].bitcast(F32)
        arow = [CBf[:, 3456:4480].rearrange("p (h l) -> p h l", l=128),
                CBf[:, 4480:5504].rearrange("p (h l) -> p h l", l=128)]
        decay = CBf[:, 5504:6528].rearrange("p (h l) -> p h l", l=128)
        ytmp = CBf[:, 6528:7040]
        ytmp2 = CBf[:, 7040:7552]
        state = CBf[:, 7552:8064]
        szt = CBf[:, 8064:8576]
        xsD = CBf[:, 8576:9088]
        AB = ARENA[:].bitcast(BF16)
        dtv3 = dtv[:].rearrange("p (c h) -> p c h", h=32)
        nacol3 = nacol[:].rearrange("p (c h) -> p c h", h=32)
        ea3 = ea[:].rearrange("p (c h) -> p c h", h=32)
        dtdte3 = dtdte[:].rearrange("p (c h) -> p c h", h=32)
        cd3 = cd[:].rearrange("p (c h) -> p c h", h=32)
        mod3p = [m[:].rearrange("p (c v) -> p c v", v=2) for m in modp]
        G13p = [m[:].rearrange("p (c v) -> p c v", v=2) for m in G1p]
        G23p = [m[:].rearrange("p (c v) -> p c v", v=2) for m in G2p]
        cur = {"par": 0}
        scv3 = scv[:].rearrange("p (v k) -> p k v", v=2)

        def ACT(out, in_, func, reads, writes, **kw):
            S.op("act", lambda e: e.activation(out=out, in_=in_, func=func, **kw), reads, writes)

        def TT(out, in0, in1, op, reads, writes, eng="dve"):
            S.op(eng, lambda e: e.tensor_tensor(out=out, in0=in0, in1=in1, op=op), reads, writes)

        def TS(out, in0, s1, s2, op0, op1, reads, writes, eng="dve"):
            if op1 is None:
                S.op(eng, lambda e: e.tensor_scalar(out=out, in0=in0, scalar1=s1, scalar2=None, op0=op0), reads, writes)
            else:
                S.op(eng, lambda e: e.tensor_scalar(out=out, in0=in0, scalar1=s1, scalar2=s2, op0=op0, op1=op1), reads, writes)

        def STT(out, in0, scalar, in1, op0, op1, reads, writes):
            S.op("dve", lambda e: e.scalar_tensor_tensor(out=out, in0=in0, scalar=scalar, in1=in1, op0=op0, op1=op1), reads, writes)

        def CP(eng, out, in_, reads, writes):
            if eng == "act":
                S.op("act", lambda e: e.copy(out=out, in_=in_), reads, writes)
            else:
                S.op(eng, lambda e: e.tensor_copy(out=out, in_=in_), reads, writes)

        def MM(out, lhsT, rhs, start, stop, reads, writes, sig):
            S.op("pe", lambda e: e.matmul(out=out, lhsT=lhsT, rhs=rhs, start=start, stop=stop), reads, writes, sig=sig)

        def TR(out, in_, ident, reads, writes, sig):
            S.op("pe", lambda e: e.transpose(out=out, in_=in_, identity=ident), reads, writes, sig=sig)

        def dump(name, ap, shape, reads):
            if name in dbg:
                d = dram("dbg_" + name, shape, kind="ExternalOutput", dt=ap.dtype)
                dbg_out[name] = d
                S.dma("sp", d, ap, reads=reads, writes=[B("dbgo", name)])

        def blk_of_tile(c):
            return 0 if c < 2 else 1 + (c - 2) // 4

        bC = B("consts")
        S.op("pool", lambda e: e.memset(ones_f[:], 1.0), writes=[bC])
        S.op("pool", lambda e: e.memset(epsc[:], EPS), writes=[bC])
        S.op("pool", lambda e: e.memset(onec[:], 1.0), writes=[bC])
        S.op("pool", lambda e: e.affine_select(out=mask_f[:], in_=ones_f[:], pattern=[[1, 128]], compare_op=ALU.is_ge,
                                               fill=0.0, base=0, channel_multiplier=-1), reads=[bC], writes=[bC])
        S.op("pool", lambda e: e.affine_select(out=mask_b[:], in_=ones_f[:], pattern=[[-1, 128]], compare_op=ALU.is_ge,
                                               fill=0.0, base=0, channel_multiplier=1), reads=[bC], writes=[bC])
        S.op("pool", lambda e: e.affine_select(out=identf[:], in_=ones_f[:], pattern=[[1, 128]], compare_op=ALU.is_equal,
                                               fill=0.0, base=0, channel_multiplier=-1), reads=[bC], writes=[bC])
        CP("dve", ident_bf[:], identf[:], [bC], [bC])
        CP("dve", ones_bf[:], ones_f[:], [bC], [bC])

        S.dma("sp", stg[0][0:16, :], cc_d, writes=[B("stg", 0)])
        bk, bb = bank()
        TR(bk[:, 0:16], stg[0][0:16, :], identf[0:16, 0:16], [B("stg", 0), bC], [bb], True)
        ACT(scf[:], bk[:, 0:16], AF.Silu, [bb], [B("scv")])
        CP("dve", scv[:], scf[:], [B("scv")], [B("scv")])
        S.dma("sp", stg[1][0:8, :], fng_d, writes=[B("stg", 1)])
        bk, bb = bank()
        TR(bk[:, 0:8], stg[1][0:8, :], identf[0:8, 0:8], [B("stg", 1), bC], [bb], True)
        CP("dve", fng[:], bk[:, 0:8], [bb], [B("fng")])

        inb = [ARENA[:, 0:1024], ARENA[:, 1024:2048]]
        for c in range(NT):
            src = ctx_d[c * 128:(c + 1) * 128, :] if c < 2 else x_d[(c - 2) * 128:(c - 1) * 128, :]
            ib = inb[c % 2]
            S.dma("sp", ib, src, writes=[B("inb", c % 2)])
            for half in range(2):
                bk, bb = bank()
                for j in range(4):
                    f = half * 4 + j
                    TR(bk[:, j * 128:(j + 1) * 128], ib[:, f * 128:(f + 1) * 128], identf[:], [B("inb", c % 2), bC], [bb], j == 3)
                CP("act" if half == 0 else "dve", Xv[:, half * 4:half * 4 + 4, c * 128:(c + 1) * 128],
                   bk[:, 0:512].rearrange("p (a b) -> p a b", b=128), [bb], [B("X", blk_of_tile(c))])
        S.barrier()

        def load_consts(i):
            bst = [B("stg", k) for k in range(4)]
            S.dma("sp", stg[0][0:48, :], adab_d[i].rearrange("(r p) -> r p", p=128), writes=[bst[0]])
            S.dma("sp", stg[0][48:56, :], n1g_d[i].rearrange("(r p) -> r p", p=128), writes=[bst[0]])
            S.dma("sp", stg[0][56:64, :], n2g_d[i].rearrange("(r p) -> r p", p=128), writes=[bst[0]])
            S.dma("sp", stg[0][64:76, :], scb_d[i].rearrange("(r p) -> r p", p=128), writes=[bst[0]])
            S.dma("sp", stg[0][76:84, :], cmb_d[i].rearrange("(r p) -> r p", p=128), writes=[bst[0]])
            S.dma("sp", stg[0][84:92, :], lng_d[i].rearrange("(r p) -> r p", p=128), writes=[bst[0]])
            S.dma("sp", stg[0][92:100, :], lnb_d[i].rearrange("(r p) -> r p", p=128), writes=[bst[0]])
            S.dma("sp", stg[1][0:60, :], scw_d[i].rearrange("k (c p) -> (k c) p", p=128), writes=[bst[1]])
            S.dma("sp", stg[2][0:128, :], cmw_d[i][0:16].rearrange("k (c p) -> (k c) p", p=128), writes=[bst[2]])
            S.dma("sp", stg[3][0:120, :], cmw_d[i][16:31].rearrange("k (c p) -> (k c) p", p=128), writes=[bst[3]])
            bk, bb = bank()
            TR(bk[:, 0:100], stg[0][0:100, :], identf[0:100, 0:100], [bst[0], bC], [bb], False)
            TR(bk[:, 100:160], stg[1][0:60, :], identf[0:60, 0:60], [bst[1], bC], [bb], False)
            TR(bk[:, 160:288], stg[2][0:128, :], identf[:], [bst[2], bC], [bb], False)
            TR(bk[:, 288:408], stg[3][0:120, :], identf[0:120, 0:120], [bst[3], bC], [bb], True)
            CP("dve", vec[:], bk[:, 0:408], [bb], [B("vec")])
            S.dma("sp", gn_bc[:], sng_d[i].partition_broadcast(128), writes=[B("gn_bc")])
            S.dma("sp", D_bc[:], dsk_d[i].partition_broadcast(128), writes=[B("smallbc")])
            S.dma("sp", dtb_bc[:], dtb_d[i].partition_broadcast(128), writes=[B("smallbc")])
            S.dma("sp", a_bc[:], alog_d[i].partition_broadcast(128), writes=[B("smallbc")])
            ACT(a_bc[:], a_bc[:], AF.Exp, [B("smallbc")], [B("smallbc")])
            TS(a_bc[:], a_bc[:], -1.0, None, ALU.mult, None, [B("smallbc")], [B("smallbc")])

        def vcol(j):
            return vec[:, j:j + 1]

        def ada_steps(i, par, arena_slots):
            wv = adaw_d[i].rearrange("(k p) n -> p k n", p=128)
            m3 = mod3p[par]
            aT = adaT[par]
            bM, bG, bT = B("mod", par), B("G", par), B("adaT", par)
            steps = []

            def setup():
                bs = B("stg", 0)
                S.dma("sp", stg[0][0:48, :], adab_d[i].rearrange("(r p) -> r p", p=128), writes=[bs])
                S.dma("sp", stg[0][48:56, :], n1g_d[i].rearrange("(r p) -> r p", p=128), writes=[bs])
                S.dma("sp", stg[0][56:64, :], n2g_d[i].rearrange("(r p) -> r p", p=128), writes=[bs])
                bk, bb = bank()
                TR(bk[:, 0:64], stg[0][0:64, :], identf[0:64, 0:64], [bs, bC], [bb], True)
                CP("dve", aT[:], bk[:, 0:64], [bb], [bT])
            steps.append(setup)

            def mk(s):
                def step():
                    if arena_slots:
                        k_ = s % 2
                        w3 = AB[:, 4096 + k_ * 4096:8192 + k_ * 4096].rearrange("p (k w) -> p k w", w=512)
                        wb = B("adaslot", k_)
                    else:
                        ws, wb = wslot()
                        w3 = ws[:, 0:4096].rearrange("p (k w) -> p k w", w=512)
                    S.dma("pool", w3, wv[:, :, s * 512:(s + 1) * 512], writes=[wb])
                    bk, bb = bank()
                    for j in range(4):
                        for k in range(8):
                            MM(bk[:, j * 2:j * 2 + 2], w3[:, k, j * 128:(j + 1) * 128], scv3[:, k, :],
                               k == 0, k == 7, [wb, B("scv")], [bb], (k == 7 and j == 3))
                    TT(m3[:, s * 4:(s + 1) * 4, :], bk[:, 0:8].rearrange("p (c v) -> p c v", v=2),
                       aT[:, s * 4:(s + 1) * 4].unsqueeze(2).to_broadcast([128, 4, 2]), ALU.add, [bb, bT], [bM])
                return step
            for s in range(12):
                steps.append(mk(s))

            def finish():
                TS(G13p[par], m3[:, 8:16, :], 1.0, None, ALU.add, None, [bM], [bG])
                TT(G13p[par], G13p[par], aT[:, 48:56].unsqueeze(2).to_broadcast([128, 8, 2]), ALU.mult, [bG, bT], [bG])
                TS(G23p[par], m3[:, 32:40, :], 1.0, None, ALU.add, None, [bM], [bG])
                TT(G23p[par], G23p[par], aT[:, 56:64].unsqueeze(2).to_broadcast([128, 8, 2]), ALU.mult, [bG, bT], [bG])
            steps.append(finish)
            return steps

        def rms_block(t0, N, src_buf):
            sq3 = AB[:, 0:4096].rearrange("p (f n) -> p f n", n=512)
            rs = ARENA[:, 2048:2560]
            sd = ARENA[:, 2560:3072]
            ACT(sq3[:, :, 0:N], Xv[:, :, t0:t0 + N], AF.Square, [src_buf], [B("sq")])
            bk, bb = bank()
            for f in range(8):
                MM(bk[:, 0:N], ones_bf[:], sq3[:, f, 0:N], f == 0, f == 7, [B("sq"), bC], [bb], f == 7)
            ACT(sd[:, 0:N], bk[:, 0:N], AF.Sqrt, [bb, bC], [B("sd")], scale=1.0 / D, bias=epsc[:])
            S.op("dve", lambda e: e.reciprocal(out=rs[:, 0:N], in_=sd[:, 0:N]), [B("sd")], [B("rs")])
            return rs

        def norm_stats(b, pb):
            t0, N = BLK[b]
            sq3 = AB[:, pb * 4096:(pb + 1) * 4096].rearrange("p (f n) -> p f n", n=512)
            rs = ARENA[:, 4096 + pb * 512:4608 + pb * 512]
            ACT(sq3[:, :, 0:N], Xv[:, :, t0:t0 + N], AF.Square, [B("X", b)], [B("sq", pb)])
            bk, bb = bank()
            for f in range(8):
                MM(bk[:, 0:N], ones_bf[:], sq3[:, f, 0:N], f == 0, f == 7, [B("sq", pb), bC], [bb], f == 7)
            ACT(rs[:, 0:N], bk[:, 0:N], AF.Sqrt, [bb, bC], [B("rs", pb)], scale=1.0 / D, bias=epsc[:])
            S.op("dve", lambda e: e.reciprocal(out=rs[:, 0:N], in_=rs[:, 0:N]), [B("rs", pb)], [B("rs", pb)])
            return rs

        def norm_apply(which, b, par, rs, pb):
            G3 = G13p[par] if which == 1 else G23p[par]
            mod3 = mod3p[par]
            sh0 = 0 if which == 1 else 24
            tmpf = [ARENA[:, 5120:5632], ARENA[:, 5632:6144]]
            t0, N = BLK[b]
            v = 1 if b == 0 else 0
            for f in range(8):
                tf = tmpf[f % 2]
                TT(tf[:, 0:N], Xv[:, f, t0:t0 + N], rs[:, 0:N], ALU.mult, [B("X", b), B("rs", pb)], [B("tmpf", f % 2)])
                ACT(HBv[:, f, t0:t0 + N], tf[:, 0:N], AF.Identity, [B("tmpf", f % 2), B("G", par), B("mod", par)], [B("HB", b)],
                    scale=G3[:, f, v:v + 1], bias=mod3[:, sh0 + f, v:v + 1])
            if which == 1:
                S.dma("sp", x_scr[:, :, t0:t0 + N], Xv[:, :, t0:t0 + N], reads=[B("X", b)], writes=[B("xscr", b)])

        def norm_block(which, b, par):
            rs = norm_stats(b, b % 2)
            norm_apply(which, b, par, rs, b % 2)

        def norm_phase(which, skip_ctx=False):
            blks = [b for b in range(5) if not (skip_ctx and b == 0)]
            par = cur["par"]
            pend = None
            for n_, b in enumerate(blks):
                rs = norm_stats(b, n_ % 2)
                if pend is not None:
                    norm_apply(which, *pend)
                pend = (pend_b := b, par, rs, n_ % 2)
            norm_apply(which, *pend)

        def dt_phase(i):
            HBall = [B("HB", b) for b in range(5)]
            ws, wb = wslot()
            w3 = ws[:, 0:256].rearrange("p (k w) -> p k w", w=32)
            S.dma("pool", w3, w_in_d[i].rearrange("(k p) n -> p k n", p=128)[:, :, 2560:2592], writes=[wb])
            bkA, bbA = bank()
            bkB, bbB = bank()
            for c in range(NT):
                bk_, bb_ = (bkA, bbA) if c < 16 else (bkB, bbB)
                for k in range(8):
                    MM(bk_[:, (c % 16) * 32:(c % 16) * 32 + 32], HBv[:, k, c * 128:(c + 1) * 128], w3[:, k, :],
                       k == 0, k == 7, [HBall[blk_of_tile(c)], wb], [bb_], k == 7 and (c == 15 or c == 17))
            dA = ARENA[:, 0:576]
            dA3 = dA.rearrange("p (c h) -> p c h", h=32)
            araw = ARENA[:, 576:1152]
            bA = B("dtA")
            TT(araw[:, 0:512].rearrange("p (c h) -> p c h", h=32), bkA[:, 0:512].rearrange("p (c h) -> p c h", h=32),
               dtb_bc[:].unsqueeze(1).to_broadcast([128, 16, 32]), ALU.add, [bbA, B("smallbc")], [bA])
            TT(araw[:, 512:576].rearrange("p (c h) -> p c h", h=32), bkB[:, 0:64].rearrange("p (c h) -> p c h", h=32),
               dtb_bc[:].unsqueeze(1).to_broadcast([128, 2, 32]), ALU.add, [bbB, B("smallbc")], [bA])
            ACT(araw, araw, AF.Exp, [bA], [bA])
            ACT(dtv[:], araw, AF.Ln, [bA, bC], [B("dt")], bias=onec[:], scale=1.0)
            TT(dA3, dtv3, a_bc[:].unsqueeze(1).to_broadcast([128, 18, 32]), ALU.mult, [B("dt"), B("smallbc")], [B("dA")])
            bkT, bbT = bank()
            bkT2, bbT2 = bank()
            MM(bkT[:, 0:512], ones_f[:], dA[:, 0:512], True, True, [B("dA"), bC], [bbT], True)
            MM(bkT2[:, 0:64], ones_f[:], dA[:, 512:576], True, True, [B("dA"), bC], [bbT2], True)
            bkC, bbC = bank()
            bkC2, bbC2 = bank()
            for c in range(NT):
                bk_, bb_ = (bkC, bbC) if c < 16 else (bkC2, bbC2)
                o = (c % 16) * 32
                MM(bk_[:, o:o + 16], mask_f[:], dA3[:, c, 0:16], True, True, [B("dA"), bC], [bb_], False)
                MM(bk_[:, o + 16:o + 32], mask_b[:], dA3[:, c, 16:32], True, True, [B("dA"), bC], [bb_], c == 15 or c == 17)
            TS(nacol[:, 0:512], bkC[:, 0:512], -1.0, None, ALU.mult, None, [bbC], [B("nacol")])
            TS(nacol[:, 512:576], bkC2[:, 0:64], -1.0, None, ALU.mult, None, [bbC2], [B("nacol")])
            ACT(ea[:], nacol[:], AF.Exp, [B("nacol")], [B("ea")], scale=-1.0)
            ACT(cd[:, 0:512], bkT[:, 0:512], AF.Exp, [bbT], [B("cd")])
            ACT(cd[:, 512:576], bkT2[:, 0:64], AF.Exp, [bbT2], [B("cd")])
            TT(dtdte[:, 0:512], bkT[:, 0:512], nacol[:, 0:512], ALU.add, [bbT, B("nacol")], [B("dtdte")])
            TT(dtdte[:, 512:576], bkT2[:, 0:64], nacol[:, 512:576], ALU.add, [bbT2, B("nacol")], [B("dtdte")])
            ACT(dtdte[:], dtdte[:], AF.Exp, [B("dtdte")], [B("dtdte")])
            TT(dtdte[:], dtdte[:], dtv[:], ALU.mult, [B("dtdte"), B("dt")], [B("dtdte")])
            ACT(dtv[:], dtv[:], AF.Ln, [B("dt")], [B("dt")])
            TT(dtv[:], dtv[:], nacol[:], ALU.add, [B("dt"), B("nacol")], [B("dt")])
            sg = [ARENA[0:16, 1152:1408], ARENA[0:16, 1408:1664]]
            for c in range(NT):
                bk_, bb_ = bank()
                MM(bk_[0:16, 0:128], dA3[:, c, 0:16], mask_f[:], True, True, [B("dA"), bC], [bb_], False)
                MM(bk_[0:16, 128:256], dA3[:, c, 16:32], mask_b[:], True, True, [B("dA"), bC], [bb_], True)
                CP("act", sg[c % 2], bk_[0:16, 0:256], [bb_], [B("sg", c % 2)])
                S.dma("sp", acs_dram[c].rearrange("d h l -> h d l"), sg[c % 2].rearrange("h (d l) -> h d l", d=2),
                      reads=[B("sg", c % 2)], writes=[B("acs", c)])

        def ssd_proj(i, g):
            HBall = [B("HB", b) for b in range(5)]
            xpb = [AB[:, 0:2312], AB[:, 2312:4624]]
            dg = [AB[:, 4624:5264].rearrange("p (k m) -> p k m", m=128), AB[:, 5264:5904].rearrange("p (k m) -> p k m", m=128)]
            sob = [AB[:, 5904:8208], AB[:, 8208:10512]]
            for p in range(2):
                S.op("dve", lambda e: e.memset(xpb[p][:, 0:2], 0.0), writes=[B("xp", p)])
                S.op("dve", lambda e: e.memset(xpb[p][:, 258:262], 0.0), writes=[B("xp", p)])
                S.op("dve", lambda e: e.memset(xpb[p][:, 2310:2312], 0.0), writes=[B("xp", p)])
            wv = w_in_d[i].rearrange("(k p) n -> p k n", p=128)
            wsA, wbA = wslot()
            wA3 = wsA[:, 0:4096].rearrange("p (k w) -> p k w", w=512)
            S.dma("pool", wA3, wv[:, :, 1024 + g * 512:1024 + (g + 1) * 512], writes=[wbA])
            wsB, wbB = wslot()
            wB3 = wsB[:, 0:2048].rearrange("p (k w) -> p k w", w=256)
            S.dma("pool", wB3[:, :, 0:128], wv[:, :, 2048 + g * 128:2048 + (g + 1) * 128], writes=[wbB])
            S.dma("pool", wB3[:, :, 128:256], wv[:, :, 2304 + g * 128:2304 + (g + 1) * 128], writes=[wbB])
            chunks = [("xs", q, wA3, wbA, q * 128, g * 4 + q) for q in range(4)]
            chunks.append(("B", 0, wB3, wbB, 0, 8 + g))
            chunks.append(("C", 0, wB3, wbB, 128, 10 + g))
            for ci, (kind, q, w3, wb, col0, qc) in enumerate(chunks):
                p = ci % 2
                xp = xpb[p]
                bxp = B("xp", p)
                for k in range(5):
                    TS(dg[p][:, k, :], ident_bf[:], vcol(100 + k * 12 + qc), None, ALU.mult, None, [bC, B("vec")], [B("dg", p)])
                for b, (t0, N) in enumerate(BLK):
                    bk, bb = bank()
                    for k in range(8):
                        MM(bk[:, 0:N], w3[:, k, col0:col0 + 128], HBv[:, k, t0:t0 + N], k == 0, k == 7, [wb, HBall[b]], [bb], k == 7)
                    off = 2 if b == 0 else 262 + (t0 - 256)
                    CP("dve", xp[:, off:off + N], bk[:, 0:N], [bb], [bxp])
                if kind == "xs":
                    dest, bdest = sob[p], B("so", p)
                elif kind == "B":
                    dest, bdest = BT, B("BT")
                else:
                    dest, bdest = CT, B("CT")
                for b, (t0, N) in enumerate(BLK):
                    bk, bb = bank()
                    base = 0 if b == 0 else 260 + (t0 - 256)
                    for k in range(5):
                        MM(bk[:, 0:N], dg[p][:, k, :], xp[:, base + k:base + k + N], k == 0, k == 4, [B("dg", p), bxp], [bb], k == 4)
                    ACT(dest[:, t0:t0 + N], bk[:, 0:N], AF.Silu, [bb, B("vec")], [bdest], bias=vcol(64 + qc), scale=1.0)
                if kind == "C":
                    continue
                for (c0, c1) in ((0, 8), (8, 16), (16, 18)):
                    bk, bb = bank()
                    bkb = bk[:].bitcast(BF16)
                    for c in range(c0, c1):
                        TR(bkb[:, (c - c0) * 128:(c - c0 + 1) * 128], dest[:, c * 128:(c + 1) * 128], ident_bf[:], [bdest, bC], [bb], c == c1 - 1)
                    n = c1 - c0
                    src3 = bkb[:, 0:n * 128].rearrange("p (a b) -> p a b", b=128)
                    if kind == "xs":
                        CP("dve", xs_tok[:, c0:c1, q * 128:(q + 1) * 128], src3, [bb], [B("xs_tok")])
                    else:
                        CP("dve", Btok[:, c0:c1, :], src3, [bb], [B("Btok")])

        def sweep(i, g, d, order, final, wz3=None, wzb=None):
            HBall = [B("HB", b) for b in range(5)]
            scT = [AB[:, 0:1024].rearrange("p (h l) -> p h l", l=128), AB[:, 1024:2048].rearrange("p (h l) -> p h l", l=128)]
            xdt = [AB[:, 2048:2560], AB[:, 2560:3072]]
            xdte = [AB[:, 3072:3584], AB[:, 3584:4096]]
            st_bf = AB[:, 4096:4608]
            CBTm = [ARENA[:, 2304:2432], ARENA[:, 2432:2560]]
            dec = [ARENA[:, 2560:3584].rearrange("p (h l) -> p h l", l=128), ARENA[:, 3584:4608].rearrange("p (h l) -> p h l", l=128)]
            xsDb = [AB[:, 9216:9728], AB[:, 9728:10240]]
            ytb = AB[:, 10240:10752]
            szts = [szt, xsD]
            identD = AB[:, 10752:11776].rearrange("p (h m) -> p h m", m=128)
            if final:
                for j in range(8):
                    TS(identD[:, j, :], ident_bf[:], D_bc[:, g * 8 + j:g * 8 + j + 1], None, ALU.mult, None,
                       [bC, B("smallbc")], [B("identD")])
            mask = mask_f if d == 0 else mask_b
            col0 = d * 16 + g * 8
            n = len(order)
            bk_cb, bb_cb = banks[0], B("bank", 0)
            bk_y = [banks[1], banks[2]]
            bk_st = [banks[3], banks[4]]
            bk_z = [banks[5], banks[6]]
            bk_yo, bb_yo = banks[7], B("bank", 7)

            def load_arow(idx):
                c = order[idx]
                S.dma("sp", arow[idx % 2], acs_dram[c, d, g * 8:(g + 1) * 8, :].partition_broadcast(128),
                      reads=[B("acs", c)], writes=[B("arow", idx % 2)])

            def h3(ap):
                return ap.rearrange("p (h e) -> p h e", e=64)

            def front_a(idx):
                c = order[idx]
                p = idx % 2
                if idx + 1 < n:
                    load_arow(idx + 1)
                ar = arow[p]
                bar = B("arow", p)
                cs = slice(c * 128, (c + 1) * 128)
                MM(bk_cb[:, 0:128], BT[:, cs], CT[:, cs], True, True, [B("BT"), B("CT")], [bb_cb], True)
                TT(CBTm[p], bk_cb[:, 0:128], mask[:], ALU.mult, [bb_cb, bC], [B("CBTm", p)])
                for j in range(8):
                    ACT(dec[p][:, j, :], ar[:, j, :], AF.Exp, [bar, B("dt")], [B("decay", p)],
                        bias=dtv3[:, c, col0 + j:col0 + j + 1], scale=1.0)
                STT(scT[p], dec[p], 1.0e30, CBTm[p].unsqueeze(1).to_broadcast([128, 8, 128]), ALU.min, ALU.mult,
                    [B("decay", p), B("CBTm", p)], [B("scT", p)])
                xs3 = h3(xs_tok[:, c, :])
                TT(h3(xdte[p]), xs3, dtdte3[:, c, col0:col0 + 8].unsqueeze(2).to_broadcast([128, 8, 64]), ALU.mult,
                   [B("xs_tok"), B("dtdte")], [B("xdte", p)])

            def front_b(idx):
                c = order[idx]
                p = idx % 2
                cs = slice(c * 128, (c + 1) * 128)
                bby = B("bank", 1 + p)
                if final:
                    MM(bk_y[p][:, 0:512], ident_bf[:], y_st[:, c, :], True, False, [bC, B("y_st", c)], [bby], False)
                    for j in range(8):
                        MM(bk_y[p][:, j * 64:(j + 1) * 64], identD[:, j, :], xs_tok[:, c, j * 64:(j + 1) * 64], False, False,
                           [B("identD"), B("xs_tok")], [bby], False)
                for j in range(8):
                    MM(bk_y[p][:, j * 64:(j + 1) * 64], scT[p][:, j, :], xs_tok[:, c, j * 64:(j + 1) * 64],
                       (not final), (j == 7 and idx == 0) if final else True,
                       [B("scT", p), B("xs_tok")], [bby], j == 7)
                MM(bk_st[p][:, 0:512], Btok[:, c, :], xdte[p], True, True, [B("Btok"), B("xdte", p)], [B("bank", 3 + p)], True)
                if final:
                    for k in range(8):
                        MM(bk_z[p][:, 0:512], HBv[:, k, cs], wz3[:, k, :], k == 0, k == 7, [HBall[blk_of_tile(c)], wzb], [B("bank", 5 + p)], k == 7)

            def back(idx):
                c = order[idx]
                p = idx % 2
                cs = slice(c * 128, (c + 1) * 128)
                bby = B("bank", 1 + p)
                bbst = B("bank", 3 + p)
                bbz = B("bank", 5 + p)
                bky = bk_y[p]
                ea_b = ea3[:, c, col0:col0 + 8].unsqueeze(2).to_broadcast([128, 8, 64])
                if idx > 0:
                    MM(bk_yo[:, 0:512], CT[:, cs], st_bf, True, True, [B("CT"), B("st_bf")], [bb_yo], True)
                    if final:
                        TT(h3(ytb), h3(bk_yo[:, 0:512]), ea_b, ALU.mult, [bb_yo, B("ea")], [B("ytb")])
                        MM(bky[:, 0:512], ident_bf[:], ytb, False, True, [bC, B("ytb")], [bby], True)
                    else:
                        TT(h3(ytmp), h3(bk_yo[:, 0:512]), ea_b, ALU.mult, [bb_yo, B("ea")], [B("ytmp")])
                if not final:
                    if idx > 0:
                        TT(y_st[:, c, :], ytmp, bky[:, 0:512], ALU.add, [B("ytmp"), bby], [B("y_st", c)])
                    else:
                        CP("dve", y_st[:, c, :], bky[:, 0:512], [bby], [B("y_st", c)])
                else:
                    szt = szts[p]
                    bsz = B("szt", p)
                    ACT(szt, bk_z[p][:, 0:512], AF.Tanh, [bbz], [bsz], scale=0.5)
                    STT(szt, szt, 1.0, bk_z[p][:, 0:512], ALU.add, ALU.mult, [bsz, bbz], [bsz])
                    STT(g_st[:, c, g * 512:(g + 1) * 512], szt, 0.5, bky[:, 0:512], ALU.mult, ALU.mult, [bby, bsz], [B("g_st", c)])
                if idx + 1 < n:
                    if idx == 0:
                        CP("dve", state, bk_st[p][:, 0:512], [bbst], [B("state")])
                    else:
                        TT(h3(state), h3(state), cd3[:, c, col0:col0 + 8].unsqueeze(2).to_broadcast([128, 8, 64]), ALU.mult,
                           [B("state"), B("cd")], [B("state")])
                        TT(state, state, bk_st[p][:, 0:512], ALU.add, [B("state"), bbst], [B("state")])
                    CP("pool", st_bf, state, [B("state")], [B("st_bf")])

            load_arow(0)
            front_a(0)
            if n > 1:
                front_a(1)
            front_b(0)
            for idx in range(n):
                if idx + 2 < n:
                    front_a(idx + 2)
                if idx + 1 < n:
                    front_b(idx + 1)
                back(idx)

        def cm_phase(i):
            NPE = 24
            HBall = [B("HB", b) for b in range(5)]
            upad = [AB[:, 0:3296], AB[:, 3296:6592]]
            diag = AB[:, 6592:6592 + NPE * 128].rearrange("p (k m) -> p k m", m=128)
            accd = ARENA[:, 4832:5344]
            sgm = ARENA[:, 5344:5856]
            S.op("dve", lambda e: e.memset(AB[:, 0:6592], 0.0), writes=[B("upad", 0), B("upad", 1)])
            wv = w_in_d[i].rearrange("(k p) n -> p k n", p=128)
            for q in range(8):
                ws, wb = wslot()
                w3 = ws[:, 0:2048].rearrange("p (k w) -> p k w", w=256)
                S.dma("pool", w3[:, :, 0:128], wv[:, :, 2592 + q * 128:2592 + (q + 1) * 128], writes=[wb])
                S.dma("pool", w3[:, :, 128:256], wv[:, :, 3616 + q * 128:3616 + (q + 1) * 128], writes=[wb])
                up = upad[q % 2]
                bup = B("upad", q % 2)
                up_ctx = up[:, 0:286]
                up_lat = up[:, 286:3294].rearrange("p (r w) -> p r w", w=94)
                for k in range(NPE):
                    TS(diag[:, k, :], ident_bf[:], vcol(160 + k * 8 + q), None, ALU.mult, None, [bC, B("vec")], [B("diag")])
                for b, (t0, N) in enumerate(BLK):
                    bka, bba = bank()
                    bkg, bbg = bank()
                    for k in range(8):
                        MM(bka[:, 0:N], w3[:, k, 0:128], HBv[:, k, t0:t0 + N], k == 0, k == 7, [wb, HBall[b]], [bba], k == 7)
                    for k in range(8):
                        MM(bkg[:, 0:N], w3[:, k, 128:256], HBv[:, k, t0:t0 + N], k == 0, k == 7, [wb, HBall[b]], [bbg], k == 7)
                    ACT(sgm[:, 0:N], bkg[:, 0:N], AF.Sigmoid, [bbg], [B("sgm")])
                    if b == 0:
                        TT(up_ctx[:, 15:271], bka[:, 0:256], sgm[:, 0:256], ALU.mult, [bba, B("sgm")], [bup])
                    else:
                        r0 = (t0 - 256) // 64
                        TT(up_lat[:, r0:r0 + 8, 15:79], bka[:, 0:512].rearrange("p (r w) -> p r w", w=64),
                           sgm[:, 0:512].rearrange("p (r w) -> p r w", w=64), ALU.mult, [bba, B("sgm")], [bup])
                for b, (t0, N) in enumerate(BLK):
                    bkv, bbv = bank()

                    def win(k):
                        if b == 0:
                            return up_ctx[:, k:k + 256]
                        r0 = (t0 - 256) // 64
                        return up_lat[:, r0:r0 + 8, k:k + 64]

                    def shp(ap):
                        return ap[:, 0:256] if b == 0 else ap[:, 0:512].rearrange("p (r w) -> p r w", w=64)
                    for k in range(NPE):
                        MM(shp(bkv), diag[:, k, :], win(k), k == 0, k == NPE - 1, [B("diag"), bup], [bbv], k == NPE - 1)
                    for k in range(NPE, 31):
                        wk = vcol(160 + k * 8 + q)
                        if k == NPE:
                            TS(shp(accd), win(k), wk, None, ALU.mult, None, [bup, B("vec")], [B("accd")])
                        else:
                            STT(shp(accd), win(k), wk, shp(accd), ALU.mult, ALU.add, [bup, B("accd"), B("vec")], [B("accd")])
                    STT(CBv[:, q, t0:t0 + N], bkv[:, 0:N], vcol(76 + q), accd[:, 0:N], ALU.add, ALU.add,
                        [bbv, B("vec"), B("accd")], [B("cm", q, b)])

        def cm_ln():
            vsq = AB[:, 0:4096].rearrange("p (f n) -> p f n", n=512)
            mean = ARENA[:, 2048:2560]
            rstd = ARENA[:, 2560:3072]
            var = ARENA[:, 3072:3584]
            tt = [ARENA[:, 3584:4096], ARENA[:, 4096:4608]]

            def ln_block(b):
                t0, N = BLK[b]
                cmall = [B("cm", q, b) for q in range(8)]
                ACT(vsq[:, :, 0:N], CBv[:, :, t0:t0 + N], AF.Square, cmall, [B("vsq")])
                bk1, bb1 = bank()
                bk2, bb2 = bank()
                for f in range(8):
                    MM(bk1[:, 0:N], ones_bf[:], CBv[:, f, t0:t0 + N], f == 0, f == 7, [cmall[f], bC], [bb1], f == 7)
                for f in range(8):
                    MM(bk2[:, 0:N], ones_bf[:], vsq[:, f, 0:N], f == 0, f == 7, [B("vsq"), bC], [bb2], f == 7)
                TS(mean[:, 0:N], bk1[:, 0:N], 1.0 / D, None, ALU.mult, None, [bb1], [B("mean")])
                TT(var[:, 0:N], mean[:, 0:N], mean[:, 0:N], ALU.mult, [B("mean")], [B("var")])
                STT(var[:, 0:N], bk2[:, 0:N], 1.0 / D, var[:, 0:N], ALU.mult, ALU.subtract, [bb2, B("var")], [B("var")])
                ACT(var[:, 0:N], var[:, 0:N], AF.Sqrt, [B("var"), bC], [B("var")], bias=epsc[:], scale=1.0)
                S.op("dve", lambda e: e.reciprocal(out=rstd[:, 0:N], in_=var[:, 0:N]), [B("var")], [B("rstd")])
                for f in range(8):
                    t_ = tt[f % 2]
                    TT(t_[:, 0:N], CBv[:, f, t0:t0 + N], mean[:, 0:N], ALU.subtract, [cmall[f], B("mean")], [B("tt", f % 2)])
                    TT(t_[:, 0:N], t_[:, 0:N], rstd[:, 0:N], ALU.mult, [B("tt", f % 2), B("rstd")], [B("tt", f % 2)])
                    ACT(CBv[:, f, t0:t0 + N], t_[:, 0:N], AF.Silu, [B("tt", f % 2), B("vec")], [cmall[f]],
                        scale=vcol(84 + f), bias=vcol(92 + f))
            return ln_block

        def ssd_final():
            sqts = [ARENA[:, 2048:3072], ARENA[:, 3072:4096]]
            for c in range(NT):
                sqt = sqts[c % 2]
                ACT(sqt, g_st[:, c, :], AF.Square, [B("g_st", c)], [B("sqt", c % 2)])
                S.op("dve", lambda e: e.reduce_sum(out=ssq[:, c:c + 1], in_=sqt, axis=AX.X), [B("sqt", c % 2)], [B("ssq", c)])
            ACT(rst[:], ssq[:], AF.Sqrt, [B("ssq", c_) for c_ in range(NT)] + [bC], [B("rst")], scale=1.0 / D, bias=epsc[:])
            S.op("dve", lambda e: e.reciprocal(out=rst[:], in_=rst[:]), [B("rst")], [B("rst")])
            ob = [AB[:, 0:1024], AB[:, 1024:2048]]
            for c in range(NT):
                o = ob[c % 2]
                STT(o, g_st[:, c, :], rst[:, c:c + 1], gn_bc[:], ALU.mult, ALU.mult, [B("g_st", c), B("rst"), B("gn_bc")], [B("ob", c % 2)])
                bk, bb = bank()
                bkb = bk[:].bitcast(BF16)
                for f in range(8):
                    TR(bkb[:, f * 128:(f + 1) * 128], o[:, f * 128:(f + 1) * 128], ident_bf[:], [B("ob", c % 2), bC], [bb], f == 7)
                CP("act", HBv[:, :, c * 128:(c + 1) * 128], bkb[:, 0:1024].rearrange("p (a b) -> p a b", b=128), [bb], [B("HB", blk_of_tile(c))])

        def x_reload():
            for b, (t0, N) in enumerate(BLK):
                S.dma("sp", Xv[:, :, t0:t0 + N], x_scr[:, :, t0:t0 + N], reads=[B("xscr", b)], writes=[B("X", b)])

        def wout_phase(i, last, ln_block):
            par = cur["par"]
            mod3 = mod3p[par]
            wv = w_out_d[i].rearrange("(k p) n -> p k n", p=128)

            def units(half):
                for fq in range(2):
                    ws, wb = wslot()
                    w3 = ws[:, 0:4096].rearrange("p (k w) -> p k w", w=512)
                    S.dma("pool", w3, wv[:, half * 8:half * 8 + 8, fq * 512:(fq + 1) * 512], writes=[wb])
                    for f4 in range(4):
                        f = fq * 4 + f4
                        for b_, (t0, N) in enumerate(BLK):
                            if last and b_ == 0:
                                continue
                            yield (w3, wb, f, f4, b_, t0, N)

            def do_unit(half, u):
                w3, wb, f, f4, b_, t0, N = u
                v = 1 if b_ == 0 else 0
                bk, bb = bank()
                for k in range(8):
                    if half == 0:
                        rhs, rb = HBv[:, k, t0:t0 + N], B("HB", b_)
                    else:
                        rhs, rb = CBv[:, k, t0:t0 + N], B("cm", k, b_)
                    MM(bk[:, 0:N], w3[:, k, f4 * 128:(f4 + 1) * 128], rhs, k == 0, k == 7, [wb, rb], [bb], k == 7)
                STT(Xv[:, f, t0:t0 + N], bk[:, 0:N], mod3[:, 16 + f, v:v + 1], Xv[:, f, t0:t0 + N], ALU.mult, ALU.add,
                    [bb, B("mod", par), B("X", b_)], [B("X", b_)])

            ua = list(units(0))
            per = (len(ua) + 4) // 5
            for b_ in range(5):
                ln_block(b_)
                for u in ua[b_ * per:(b_ + 1) * per]:
                    do_unit(0, u)
            for u in units(1):
                do_unit(1, u)

        def ffn_phase(i, last, hooks=(), next_norm=None):
            par = cur["par"]
            mod3 = mod3p[par]
            hooks = list(hooks)

            def hook():
                if hooks:
                    hooks.pop(0)()
            sa = [ARENA[:, 0:512], ARENA[:, 512:1024]]
            first = [True]
            w1v = w1_d[i].rearrange("(k p) n -> p k n", p=128)
            w2v = w2_d[i].rearrange("(j p) n -> p j n", p=128)
            cnt = 0
            for (j0, j1) in ((0, 8), (8, 15), (15, 22)):
                nj = j1 - j0
                for jj in range(j0, j1, 2):
                    w = min(2, j1 - jj)
                    ws, wb = wslot()
                    w3 = ws[:, 0:8 * 2 * w * 128].rearrange("p (k w) -> p k w", w=2 * w * 128)
                    S.dma("pool", w3[:, :, 0:w * 128], w1v[:, :, jj * 128:(jj + w) * 128], writes=[wb])
                    S.dma("pool", w3[:, :, w * 128:2 * w * 128], w1v[:, :, FFH + jj * 128:FFH + (jj + w) * 128], writes=[wb])
                    if first[0]:
                        first[0] = False
                        if hooks:
                            S._wait("pool", S.all_events(with_pool=False))
                    for jw in range(w):
                        j = jj + jw
                        for b, (t0, N) in enumerate(BLK):
                            if last and b == 0:
                                continue
                            bka, bba = bank()
                            bkb_, bbb = bank()
                            for k in range(8):
                                MM(bka[:, 0:N], w3[:, k, jw * 128:(jw + 1) * 128], HBv[:, k, t0:t0 + N], k == 0, k == 7, [wb, B("HB", b)], [bba], k == 7)
                            for k in range(8):
                                MM(bkb_[:, 0:N], w3[:, k, (w + jw) * 128:(w + jw + 1) * 128], HBv[:, k, t0:t0 + N], k == 0, k == 7, [wb, B("HB", b)], [bbb], k == 7)
                            s_ = sa[cnt % 2]
                            ACT(s_[:, 0:N], bka[:, 0:N], AF.Silu, [bba], [B("sa", cnt % 2)])
                            TT(CBv[:, j - j0, t0:t0 + N], s_[:, 0:N], bkb_[:, 0:N], ALU.mult, [B("sa", cnt % 2), bbb], [B("hid", j - j0, b)])
                            cnt += 1
                    hook()
                if j1 == 22:
                    S.barrier()
                    wq = []
                    for half in range(2):
                        ws, wb = wslot()
                        w4 = ws[:, 0:2 * nj * 256].rearrange("p (s j w) -> p s j w", s=2, w=256)
                        for s_ in range(2):
                            fp = half * 2 + s_
                            S.dma("pool", w4[:, s_, :, :], w2v[:, j0:j1, fp * 256:(fp + 1) * 256], writes=[wb])
                            wq.append((w4[:, s_, :, :], wb))
                    for b, (t0, N) in enumerate(BLK):
                        if last and b == 0:
                            continue
                        v = 1 if b == 0 else 0
                        for f in range(8):
                            w3, wb = wq[f // 2]
                            f2 = f % 2
                            bk, bb = bank()
                            for jl in range(nj):
                                MM(bk[:, 0:N], w3[:, jl, f2 * 128:(f2 + 1) * 128], CBv[:, jl, t0:t0 + N], jl == 0, jl == nj - 1,
                                   [wb, B("hid", jl, b)], [bb], jl == nj - 1)
                            STT(Xv[:, f, t0:t0 + N], bk[:, 0:N], mod3[:, 40 + f, v:v + 1], Xv[:, f, t0:t0 + N], ALU.mult, ALU.add,
                                [bb, B("mod", par), B("X", b)], [B("X", b)])
                        if next_norm is not None:
                            next_norm(b)
                    S.barrier()
                    continue
                for fp in range(4):
                    ws, wb = wslot()
                    w3 = ws[:, 0:nj * 256].rearrange("p (j w) -> p j w", w=256)
                    S.dma("pool", w3, w2v[:, j0:j1, fp * 256:(fp + 1) * 256], writes=[wb])
                    for f2 in range(2):
                        f = fp * 2 + f2
                        for b, (t0, N) in enumerate(BLK):
                            if last and b == 0:
                                continue
                            v = 1 if b == 0 else 0
                            bk, bb = bank()
                            for jl in range(nj):
                                MM(bk[:, 0:N], w3[:, jl, f2 * 128:(f2 + 1) * 128], CBv[:, jl, t0:t0 + N], jl == 0, jl == nj - 1,
                                   [wb, B("hid", jl, b)], [bb], jl == nj - 1)
                            STT(Xv[:, f, t0:t0 + N], bk[:, 0:N], mod3[:, 40 + f, v:v + 1], Xv[:, f, t0:t0 + N], ALU.mult, ALU.add,
                                [bb, B("mod", par), B("X", b)], [B("X", b)])
                    hook()
                if j1 == 15:
                    while hooks:
                        hook()
                S.barrier()

        def final_phase():
            tbs = [CBf[:, 0:4096].rearrange("p (f n) -> p f n", n=512), CBf[:, 4096:8192].rearrange("p (f n) -> p f n", n=512)]
            ot = [ARENA[:, 4096:5120], ARENA[:, 5120:6144]]
            for b, (t0, N) in enumerate(BLK):
                if b == 0:
                    continue
                rs = rms_block(t0, N, B("X", b))
                tb = tbs[b % 2]
                for f in range(8):
                    STT(tb[:, f, :], Xv[:, f, t0:t0 + N], fng[:, f:f + 1], rs[:, 0:N], ALU.mult, ALU.mult,
                        [B("X", b), B("rs"), B("fng")], [B("tb", b % 2)])
                for cl in range(4):
                    c = (t0 // 128) + cl
                    o = ot[c % 2]
                    for half in range(2):
                        bk, bb = bank()
                        for j in range(4):
                            TR(bk[:, j * 128:(j + 1) * 128], tb[:, half * 4 + j, cl * 128:(cl + 1) * 128], identf[:],
                               [B("tb", b % 2), bC], [bb], j == 3)
                        CP("act" if half == 0 else "dve", o[:, half * 512:(half + 1) * 512], bk[:, 0:512], [bb], [B("ot", c % 2)])
                    S.dma("sp", out_d[(c - 2) * 128:(c - 1) * 128, :], o, reads=[B("ot", c % 2)], writes=[B("outd", c)])

        for st_ in ada_steps(0, 0, False):
            st_()
        for i in range(NL):
            last = (i == NL - 1)
            cur["par"] = i % 2
            load_consts(i)
            norm_phase(1)
            S.barrier()
            dump("hT%d" % i, HB[:], [128, 8 * T], [B("HB", b) for b in range(5)])
            dt_phase(i)
            dump("dtv%d" % i, dtv[:], [128, 576], [B("dt")])
            dump("nacol%d" % i, nacol[:], [128, 576], [B("nacol")])
            S.barrier()
            for g in range(2):
                ssd_proj(i, g)
                if g == 0:
                    dump("xs_tok%d" % i, XB[:, 0:9216], [128, 9216], [B("xs_tok")])
                    dump("BT%d" % i, BT, [128, 2304], [B("BT")])
                    dump("CT%d" % i, CT, [128, 2304], [B("CT")])
                S.barrier(pool_waits=True)
                sweep(i, g, 1, BWD_ORDER, False)
                wz, wzb = wslot()
                wz3 = wz[:, 0:4096].rearrange("p (k w) -> p k w", w=512)
                S.dma("pool", wz3, w_in_d[i].rearrange("(k p) n -> p k n", p=128)[:, :, g * 512:(g + 1) * 512], writes=[wzb])
                sweep(i, g, 0, FWD_ORDER, True, wz3, wzb)
                S.barrier()
            dump("g_st%d" % i, XB[:, 18432:36864], [128, 18432], [B("g_st", c) for c in range(NT)])
            cm_phase(i)
            S.barrier()
            ssd_final()
            dump("ssdT%d" % i, HB[:], [128, 8 * T], [B("HB", b) for b in range(5)])
            S.barrier()
            x_reload()
            wout_phase(i, last, cm_ln())
            dump("cm%d" % i, CB[:], [128, 8 * T], [B("cm", q, b) for q in range(8) for b in range(5)])
            S.barrier()
            dump("x1_%d" % i, X[:], [128, 8 * T], [B("X", b) for b in range(5)])
            norm_phase(2, skip_ctx=last)
            S.barrier()
            ffn_phase(i, last, ada_steps(i + 1, (i + 1) % 2, True) if not last else (),
                      None)
            dump("x2_%d" % i, X[:], [128, 8 * T], [B("X", b) for b in range(5)])
        S.barrier()
        final_phase()
        S.final()
        print("[kernel] ops=%d waits=%d" % (S.nops, S.nwaits))
    return nc, dbg_out


_W_NAMES = ["w_in", "ssd_conv_w", "ssd_conv_b", "dt_bias", "a_log", "d_skip", "ssd_norm_g", "cm_dw_w", "cm_dw_b",
            "cm_ln_g", "cm_ln_b", "w_out", "w_ffn_in", "w_ffn_out", "ada_w", "ada_b", "norm1_g", "norm2_g"]


def make_in_maps(inputs, NL=4):
    f = lambda a: np.ascontiguousarray(np.asarray(a, dtype=np.float32))
    shared = {}
    for n in _W_NAMES:
        a = f(inputs[n])[:NL]
        if n in ("dt_bias", "a_log"):
            a = a.reshape(NL, 32)
        shared[n] = np.ascontiguousarray(a)
    shared["final_norm_g"] = f(inputs["final_norm_g"]).reshape(8, 128)
    x = f(inputs["x"])
    ctx = f(inputs["ctx"])
    c = f(inputs["c"])
    c_ctx = f(inputs["c_ctx"])
    maps = []
    for b in range(8):
        m = dict(shared)
        m["x"] = x[b]
        m["ctx"] = ctx[b]
        m["cc"] = np.ascontiguousarray(np.concatenate([c[b].reshape(8, 128), c_ctx.reshape(8, 128)], axis=0))
        maps.append(m)
    return maps


_CACHE = {}


def kernel(**inputs):
    if "nc" not in _CACHE:
        _CACHE["nc"] = build_program(4)[0]
    nc = _CACHE["nc"]
    maps = make_in_maps(inputs, 4)
    res = run_bass_kernel_spmd(nc, maps, core_ids=list(range(8)))
    out = np.stack([np.asarray(r["out"], dtype=np.float32) for r in res.results], axis=0)
    return out
```
